# Optimizing a Trainium2 kernel written in Bass

```python
import math
import jax, jax.numpy as jnp
from jax import lax
import numpy as np

D_MODEL = 1024
BATCH = 2
SEQ = 8192
DEPTH = 1

HEAD_DIM = 64
N_Q_HEADS = 8
N_KV_HEADS = 2
GROUP = N_Q_HEADS // N_KV_HEADS
ATTN_WIDTH = N_Q_HEADS * HEAD_DIM
KV_WIDTH = N_KV_HEADS * HEAD_DIM
WINDOW = 128
BLOCK = 128
ROT_DIM = HEAD_DIM // 4
ROPE_THETA = 500000.0
CONV_WIDTH = D_MODEL - ATTN_WIDTH
CONV_K = 3
PROJ_WIDTH = ATTN_WIDTH + 2 * KV_WIDTH + 3 * CONV_WIDTH
MIX_WIDTH = ATTN_WIDTH + CONV_WIDTH
N_KEYS = 128
N_EXPERTS = N_KEYS * N_KEYS
PEER_HEADS = 8
PEER_QDIM = 256
PEER_HALF = PEER_QDIM // 2
PEER_TOPK = 16
PEER_CHUNK = 128
DN_ALPHA = (2.0 * DEPTH) ** 0.25
DN_BETA = (8.0 * DEPTH) ** -0.25
LN_EPS = 1e-5

kernel_name = "hymba_swa_sink_shortconv_peer_deepnorm"


def layer_norm(x, g, b):
    xf = x.astype(jnp.float32)
    mu = jnp.mean(xf, axis=-1, keepdims=True)
    xc = xf - mu
    var = jnp.mean(xc * xc, axis=-1, keepdims=True)
    return (xc * lax.rsqrt(var + LN_EPS) * g + b).astype(x.dtype)


def rope_tables(positions, dtype):
    inv_freq = ROPE_THETA ** (-jnp.arange(0, ROT_DIM, 2, dtype=jnp.float32) / ROT_DIM)
    ang = positions.astype(jnp.float32)[..., None] * inv_freq
    return jnp.cos(ang)[:, :, None, :].astype(dtype), jnp.sin(ang)[:, :, None, :].astype(dtype)


def partial_rope(t, cos, sin):
    r1 = t[..., :ROT_DIM // 2]
    r2 = t[..., ROT_DIM // 2:ROT_DIM]
    return jnp.concatenate([r1 * cos - r2 * sin, r2 * cos + r1 * sin, t[..., ROT_DIM:]], axis=-1)


def sliding_window_attention_sinks(q, k, v, sinks):
    B, S = q.shape[0], q.shape[1]
    nb = S // BLOCK
    qb = q.reshape(B, nb, BLOCK, N_KV_HEADS, GROUP, HEAD_DIM)
    pad = ((0, 0), (BLOCK, 0), (0, 0), (0, 0))
    kp = jnp.pad(k, pad).reshape(B, nb + 1, BLOCK, N_KV_HEADS, HEAD_DIM)
    vp = jnp.pad(v, pad).reshape(B, nb + 1, BLOCK, N_KV_HEADS, HEAD_DIM)
    kc = jnp.concatenate([kp[:, :-1], kp[:, 1:]], axis=2)
    vc = jnp.concatenate([vp[:, :-1], vp[:, 1:]], axis=2)
    s = jnp.einsum('bnqhgd,bnkhd->bnhgqk', qb, kc).astype(jnp.float32) * (HEAD_DIM ** -0.5)
    qi = jnp.arange(BLOCK)[:, None]
    kj = jnp.arange(2 * BLOCK)[None, :]
    diff = qi + BLOCK - kj
    band = (diff >= 0) & (diff < WINDOW)
    key_abs = jnp.arange(nb)[:, None, None] * BLOCK + kj[None] - BLOCK
    mask = band[None] & (key_abs >= 0)
    s = jnp.where(mask[None, :, None, None], s, -jnp.inf)
    sink = sinks.astype(jnp.float32).reshape(N_KV_HEADS, GROUP)[None, None, :, :, None, None]
    m = jnp.maximum(jnp.max(s, axis=-1, keepdims=True), sink)
    p = jnp.exp(s - m)
    p = p / (jnp.sum(p, axis=-1, keepdims=True) + jnp.exp(sink - m))
    o = jnp.einsum('bnhgqk,bnkhd->bnqhgd', p.astype(v.dtype), vc)
    return o.reshape(B, S, ATTN_WIDTH)


def causal_depthwise_conv3(u, w):
    up = jnp.pad(u, ((0, 0), (CONV_K - 1, 0), (0, 0)))
    return up[:, :-2] * w[0] + up[:, 1:-1] * w[1] + up[:, 2:] * w[2]


def hybrid_mixer(h, cos, sin, w_in, b_in, attn_sinks, conv_w, w_out, b_out):
    B, S, _ = h.shape
    proj = h @ w_in + b_in
    o1 = ATTN_WIDTH
    o2 = o1 + KV_WIDTH
    o3 = o2 + KV_WIDTH
    o4 = o3 + CONV_WIDTH
    o5 = o4 + CONV_WIDTH
    q = proj[..., :o1].reshape(B, S, N_Q_HEADS, HEAD_DIM)
    k = proj[..., o1:o2].reshape(B, S, N_KV_HEADS, HEAD_DIM)
    v = proj[..., o2:o3].reshape(B, S, N_KV_HEADS, HEAD_DIM)
    gate_b = proj[..., o3:o4]
    gate_c = proj[..., o4:o5]
    xc = proj[..., o5:]
    q = partial_rope(q, cos, sin)
    k = partial_rope(k, cos, sin)
    y_attn = sliding_window_attention_sinks(q, k, v, attn_sinks)
    y_conv = gate_b * causal_depthwise_conv3(gate_c * xc, conv_w)
    y = jnp.concatenate([y_attn, y_conv], axis=-1)
    return y @ w_out + b_out


def peer_ffn(h, w_pq, sub_keys1, sub_keys2, u_experts, v_experts):
    B, S, D = h.shape
    T = B * S
    xt = h.reshape(T, D)
    q = (xt @ w_pq).reshape(T, PEER_HEADS, PEER_QDIM)
    s1 = jnp.einsum('thd,nd->thn', q[..., :PEER_HALF], sub_keys1)
    s2 = jnp.einsum('thd,nd->thn', q[..., PEER_HALF:], sub_keys2)
    v1, i1 = lax.top_k(s1, PEER_TOPK)
    v2, i2 = lax.top_k(s2, PEER_TOPK)
    cand = (v1[..., :, None] + v2[..., None, :]).reshape(T, PEER_HEADS, PEER_TOPK * PEER_TOPK)
    sc, pos = lax.top_k(cand, PEER_TOPK)
    e = (jnp.take_along_axis(i1, pos // PEER_TOPK, axis=-1) * N_KEYS
         + jnp.take_along_axis(i2, pos % PEER_TOPK, axis=-1))
    g = jax.nn.softmax(sc.astype(jnp.float32), axis=-1).astype(h.dtype)
    nc = T // PEER_CHUNK
    kk = PEER_HEADS * PEER_TOPK
    xs = xt.reshape(nc, PEER_CHUNK, D)
    es = e.reshape(nc, PEER_CHUNK, kk)
    gs = g.reshape(nc, PEER_CHUNK, kk)

    def chunk(args):
        xc, ec, gc = args
        u = jnp.take(u_experts, ec, axis=0)
        a = jax.nn.gelu(jnp.einsum('tkd,td->tk', u, xc), approximate=False)
        vv = jnp.take(v_experts, ec, axis=0)
        return jnp.einsum('tk,tkd->td', gc * a, vv)

    out = lax.map(chunk, (xs, es, gs))
    return out.reshape(B, S, D)


def setup_inputs(seed: int = 0) -> dict:
    key = jax.random.key(seed)
    ks = jax.random.split(key, 20)
    f32 = jnp.float32
    nrm = lambda k, shape, scale: jax.random.normal(k, shape, f32) * scale
    x = jax.random.normal(ks[0], (BATCH, SEQ, D_MODEL), f32)
    offs = jax.random.randint(ks[1], (BATCH, 1), 0, 1024, dtype=jnp.int32)
    positions = (jnp.arange(SEQ, dtype=jnp.int32)[None, :] + offs).astype(jnp.int32)
    return {
        "x": x,
        "positions": positions,
        "w_in": nrm(ks[2], (DEPTH, D_MODEL, PROJ_WIDTH), D_MODEL ** -0.5),
        "b_in": nrm(ks[3], (DEPTH, PROJ_WIDTH), 0.02),
        "attn_sinks": nrm(ks[4], (DEPTH, N_Q_HEADS), 0.5),
        "conv_w": nrm(ks[5], (DEPTH, CONV_K, CONV_WIDTH), CONV_K ** -0.5),
        "w_out": nrm(ks[6], (DEPTH, MIX_WIDTH, D_MODEL), MIX_WIDTH ** -0.5 * DN_BETA),
        "b_out": nrm(ks[7], (DEPTH, D_MODEL), 0.02),
        "ln1_g": 1.0 + nrm(ks[8], (DEPTH, D_MODEL), 0.02),
        "ln1_b": nrm(ks[9], (DEPTH, D_MODEL), 0.02),
        "w_pq": nrm(ks[10], (DEPTH, D_MODEL, PEER_HEADS * PEER_QDIM), D_MODEL ** -0.5),
        "sub_keys1": nrm(ks[11], (DEPTH, N_KEYS, PEER_HALF), PEER_HALF ** -0.5),
        "sub_keys2": nrm(ks[12], (DEPTH, N_KEYS, PEER_HALF), PEER_HALF ** -0.5),
        "u_experts": nrm(ks[13], (DEPTH, N_EXPERTS, D_MODEL), D_MODEL ** -0.5),
        "v_experts": nrm(ks[14], (DEPTH, N_EXPERTS, D_MODEL), (PEER_HEADS * PEER_TOPK) ** -0.5 * DN_BETA),
        "ln2_g": 1.0 + nrm(ks[15], (DEPTH, D_MODEL), 0.02),
        "ln2_b": nrm(ks[16], (DEPTH, D_MODEL), 0.02),
    }


def reference(x, positions, w_in, b_in, attn_sinks, conv_w, w_out, b_out, ln1_g, ln1_b,
              w_pq, sub_keys1, sub_keys2, u_experts, v_experts, ln2_g, ln2_b):
    cos, sin = rope_tables(positions, x.dtype)
    h = x
    for l in range(DEPTH):
        mix = hybrid_mixer(h, cos, sin, w_in[l], b_in[l], attn_sinks[l], conv_w[l], w_out[l], b_out[l])
        h = layer_norm(DN_ALPHA * h + mix, ln1_g[l], ln1_b[l])
        ff = peer_ffn(h, w_pq[l], sub_keys1[l], sub_keys2[l], u_experts[l], v_experts[l])
        h = layer_norm(DN_ALPHA * h + ff, ln2_g[l], ln2_b[l])
    return h
```

```python
import contextlib
import numpy as np
import concourse.bass as bass
import concourse.mybir as mybir
from concourse.bass_utils import run_bass_kernel_spmd

F32 = mybir.dt.float32
BF16 = mybir.dt.bfloat16
I32 = mybir.dt.int32
U32 = mybir.dt.uint32
ALU = mybir.AluOpType
AF = mybir.ActivationFunctionType
AX = mybir.AxisListType

NCORES = 8
D = 1024
TPC = 2048
NBLK = TPC // 128
WIN_COLS = 2432
NEXP = 16384
ALPHA = float(2.0 ** 0.25)
LN_EPS = 1e-5
NEG = -30000.0
TWO_PI = 6.283185307179586
CW1 = 6.28125
CW2 = TWO_PI - CW1
PI_SAFE = 3.1415925

NG = 10
NDR = 6


class DSem:
    def __init__(self, sem, group=False):
        self.sem = sem
        self.count = 0
        self.group = group


class Op:
    __slots__ = ("eng", "fn", "r", "w", "dsem", "val", "idx", "waits", "name", "xw")

    def __init__(self, eng, fn, r, w, dsem, name):
        self.eng, self.fn, self.r, self.w, self.dsem, self.name = eng, fn, tuple(r), tuple(w), dsem, name
        self.val = None
        self.idx = None
        self.waits = None
        self.xw = ()


class Sched:
    def __init__(self):
        self.cur = None

    def begin(self):
        self.cur = []
        return self.cur

    def op(self, eng, fn, r=(), w=(), dsem=None, name="", xw=()):
        o = Op(eng, fn, r, w, dsem, name)
        o.xw = tuple(xw)
        self.cur.append(o)
        return o


HOP_BONUS = 1.0
ENG_W = {"dve": 1.0, "act": 0.6, "pe": 0.1, "sp": 20.0, "pool": 0.05}


def merge_lists(la, lb, frac=0.97):
    if not lb:
        return list(la)
    if not la:
        return list(lb)
    items = []
    nb = len(lb)
    for j, o in enumerate(lb):
        items.append((j + 0.5, 1, j, o))
    def wof(i):
        o = la[i]
        w = ENG_W[o.eng]
        if i > 0 and o.eng in ("dve", "act") and la[i - 1].eng != o.eng and la[i - 1].eng in ("dve", "act", "pe"):
            w += HOP_BONUS
        return w
    ws = [wof(i) for i in range(len(la))]
    tot = sum(ws)
    acc = 0.0
    for i, o in enumerate(la):
        w = ws[i]
        items.append(((acc + 0.5 * w) / tot * frac * nb, 0, i, o))
        acc += w
    items.sort(key=lambda t: (t[0], t[1], t[2]))
    return [t[3] for t in items]


def resolve(seq, eng_sems):
    last_writer = {}
    readers = {}
    counts = {e: 0 for e in eng_sems}
    for o in seq:
        deps = []
        for k in o.r:
            lw = last_writer.get(k)
            if lw is not None:
                deps.append(lw)
        for k in o.w:
            lw = last_writer.get(k)
            if lw is not None:
                deps.append(lw)
            deps.extend(readers.get(k, ()))
        waits = {}
        for d in deps:
            if d is o:
                continue
            if d.dsem is not None:
                key = id(d.dsem)
                if key not in waits or waits[key][1] < d.val:
                    waits[key] = (d.dsem.sem, d.val, d.dsem if d.dsem.group else None)
            else:
                if d.eng == "pe" and o.eng == "pe":
                    continue
                key = d.eng
                if key not in waits or waits[key][1] < d.idx:
                    waits[key] = (eng_sems[d.eng], d.idx, None)
        o.waits = list(waits.values()) + [(d_.sem, v_, None) for (d_, v_) in o.xw]
        if o.dsem is not None:
            o.dsem.count += 16
            o.val = o.dsem.count
        else:
            counts[o.eng] += 1
            o.idx = counts[o.eng]
        for k in o.r:
            readers.setdefault(k, []).append(o)
        for k in o.w:
            last_writer[k] = o
            readers[k] = []


def build_program(nblk=NBLK, taps=()):
    nc = bass.Bass("TRN2", target_bir_lowering=False)
    es = contextlib.ExitStack()

    def din(name, shape, dt=F32):
        return nc.dram_tensor(name, list(shape), dt, kind="ExternalInput").ap()

    nrows = (nblk + 1) * 128
    xh = din("xh", [nrows, D])
    pos_d = din("pos", [128, NBLK + 1], I32)
    w_in_d = din("w_in_r", [D, WIN_COLS])
    btm_d = din("b_tm", [1, 896])
    bconv_d = din("b_conv", [128, 12])
    cw_d = din("cw", [128, 12])
    sinks_d = din("sinks", [8])
    w_out_d = din("w_out", [D, D])
    bout_d = din("b_out", [1, D])
    vecs_d = din("vecs", [4, D])
    w_pq_d = din("w_pq", [D, 2048])
    skT_d = din("skT", [128, 256])
    uv_d = din("uv_exp", [NEXP, 2 * D])
    uvb_d = nc.dram_tensor("uvb", [NEXP, 2 * D], BF16, kind="Internal").ap()
    ident_d = din("ident", [128, 128])
    masks_d = din("masks", [128, 384])
    cst_d = din("cst", [128, 64])
    y_d = nc.dram_tensor("y", [nblk * 128, D], F32, kind="ExternalOutput").ap()
    tap_d = {}
    for (tname, shape, dt) in taps:
        tap_d[tname] = nc.dram_tensor("tap_" + tname, list(shape), dt, kind="ExternalOutput").ap()

    def sb(name, shape, dt=F32):
        return es.enter_context(nc.sbuf_tensor(name, list(shape), dt))

    w_in_sb = sb("w_in_sb", [128, 8, WIN_COLS], BF16)
    w_out_sb = sb("w_out_sb", [128, 8, D], BF16)
    w_pq_sb = sb("w_pq_sb", [128, 8, 2048], BF16)
    skT_sb = sb("skT_sb", [128, 2, 128], BF16)
    ident_bf = sb("ident_bf", [128, 128], BF16)
    masks_bf = sb("masks_bf", [128, 3, 128], BF16)
    btm_bf = sb("btm_bf", [1, 896], BF16)
    bout_bf = sb("bout_bf", [1, D], BF16)
    ones_bf = sb("ones_bf", [1, 128], BF16)
    vecs_sb = sb("vecs_sb", [128, 4, D], F32)
    bconv_sb = sb("bconv_sb", [128, 12], F32)
    cw_sb = sb("cw_sb", [128, 12], F32)
    sinks_sb = sb("sinks_sb", [128, 8], F32)
    esink = sb("esink", [128, 8], F32)
    cst_sb = sb("cst_sb", [128, 64], F32)
    pos_i = sb("pos_i", [128, NBLK + 1], I32)
    posf = sb("posf", [128, NBLK + 1], F32)
    NB8 = (NBLK + 1) * 8
    sin_t = sb("sin_t", [128, NBLK + 1, 8], F32)
    cos_t = sb("cos_t", [128, NBLK + 1, 8], F32)
    eps_t = sb("eps_t", [128, 1], F32)

    thr_bf = sb("thr_bf", [128, 32], BF16)
    invf = cst_sb[:, 0:8]
    thrA = thr_bf[:, 0:16]
    thrB = thr_bf[:, 16:32]
    flag = cst_sb[:, 40:41]

    guv = [sb(f"guv{i}", [128, 2 * D], BF16) for i in range(NG)]
    _sflat = guv[0][:, :].bitcast(F32)
    ang = _sflat[:, 0:NB8].rearrange("p (b f) -> p b f", f=8)
    rtmp = _sflat[:, 256:256 + NB8].rearrange("p (b f) -> p b f", f=8)
    rtmp2 = _sflat[:, 512:512 + NB8].rearrange("p (b f) -> p b f", f=8)
    rki = _sflat[:, 768:768 + NB8].bitcast(I32).rearrange("p (b f) -> p b f", f=8)
    xf = [sb(f"xf{i}", [128, D], F32) for i in range(1)]
    xT = sb("xT", [128, 8, 128], BF16)
    qk_sb = sb("qk_sb", [128, 768], F32)
    qk_rot = sb("qk_rot", [128, 768], BF16)
    rp = [sb(f"rp{i}", [128, 12, 8], F32) for i in range(4)]
    qT = sb("qT", [128, 4, 128], BF16)
    kT = [sb(f"kT{i}", [128, 2, 128], BF16) for i in range(2)]
    v_ext = [sb(f"vext{i}", [128, 2, 72], BF16) for i in range(2)]
    pT = sb("pT", [128, 16, 128], BF16)
    den_sb = sb("den_sb", [128, 4], F32)
    rden = sb("rden", [128, 4], F32)
    yatt = sb("yatt", [128, 8, 64], BF16)
    yT = sb("yT", [128, 8, 128], BF16)
    gc_sb = sb("gc_sb", [128, 2, 128], F32)
    gb_sb = sb("gb_sb", [128, 2, 128], F32)
    ubuf = [sb(f"ubuf{i}", [128, 130], F32) for i in range(4)]
    cacc = sb("cacc", [128, 2, 128], F32)
    st1 = sb("st1", [128, 12], F32)
    mv1 = sb("mv1", [128, 2], F32)
    sd1 = sb("sd1", [128, 1], F32)
    rstd1 = sb("rstd1", [128, 1], F32)
    nmr1 = sb("nmr1", [128, 1], F32)
    h1 = [sb(f"h1_{i}", [128, D], F32) for i in range(2)]
    h1b1 = sb("h1b1", [128, D], BF16)
    xb = h1b1
    h1T = xT
    qpT = pT
    vA = sb("vA", [128, 16, 16], F32)
    iA = sb("iA", [128, 16, 16], U32)
    sc = sb("sc", [128, 8, 16], F32)
    posu = sb("posu", [128, 8, 16], U32)
    pposf = sb("pposf", [128, 128], BF16)
    i12f = sb("i12f", [128, 16, 16], BF16)
    d12 = sb("d12", [128, 16, 16], BF16)

    _ptf = pT[:, :, :].rearrange("p a b -> p (a b)")
    age = _ptf[:, 0:1024].rearrange("p (k a) -> p k a", a=16)
    prod = _ptf[:, 1024:2048].rearrange("p (k a) -> p k a", a=16)
    sumA = sb("sumA", [128, 64], F32)
    bq = sb("bq", [128, 64], BF16)
    isel = sb("isel", [128, 2, 128], F32)
    ef = sb("ef", [128, 128], F32)
    e_i32 = [sb(f"e_i32_{i}", [128, 128], I32) for i in range(2)]
    scs = sb("scs", [128, 8, 16], F32)
    pex = sb("pex", [128, 8, 16], F32)
    ssum = sb("ssum", [128, 8], F32)
    rsum = sb("rsum", [128, 8], F32)
    gw = [sb(f"gw{i}", [128, 8, 16], F32) for i in range(2)]

    a_sb = sb("a_sb", [128, 128], F32)
    ga_sb = sb("ga_sb", [128, 128], F32)
    coef = sb("coef", [128, 128], F32)
    dg = [sb(f"dg{i}", [128, 128], BF16) for i in range(NDR)]
    r2 = sb("r2", [128, D], F32)
    st2 = sb("st2", [128, 12], F32)
    mv2 = sb("mv2", [128, 2], F32)
    sd2 = sb("sd2", [128, 1], F32)
    rstd2 = sb("rstd2", [128, 1], F32)
    nmr2 = sb("nmr2", [128, 1], F32)

    ps = es.enter_context(nc.psum_tensor("ps", [128, 4096], F32))
    tp_bf = ps[:, 0:512].bitcast(BF16)

    def bank(k, c0=0, n=512):
        return ps[:, k * 512 + c0: k * 512 + c0 + n]

    def newsem(name):
        return es.enter_context(nc.semaphore(name))

    eng_sems = {e: newsem("sem_" + e) for e in ("pe", "act", "dve", "pool")}
    eng_sems["sp"] = None
    ds_setup_pool = DSem(newsem("ds_setup_pool"), group=True)
    ds_w_in = DSem(newsem("ds_w_in"), group=True)
    ds_w_out = DSem(newsem("ds_w_out"), group=True)
    ds_w_pq = DSem(newsem("ds_w_pq"), group=True)
    ds_setup_sp = DSem(newsem("ds_setup_sp"), group=True)
    ds_x = [DSem(newsem(f"ds_x{i}")) for i in range(1)]
    ds_g = [DSem(newsem(f"ds_g{i}")) for i in range(NG)]
    NPREP = 16
    ds_prep = [DSem(newsem(f"ds_prep{i}")) for i in range(NPREP)]
    ds_out = DSem(newsem("ds_out"))
    ds_tap = DSem(newsem("ds_tap"))

    S = Sched()

    L_setup = S.begin()
    wv = w_in_d.rearrange("(c p) n -> p c n", p=128)
    for c in range(8):
        for (c0, c1) in ((0, 1216), (1216, 2432)):
            S.op("pool", lambda e, c=c, c0=c0, c1=c1: e.dma_start(out=w_in_sb[:, c, c0:c1], in_=wv[:, c, c0:c1]),
                 w=[("w_in", c, c0)], dsem=ds_w_in)
    S.op("pool", lambda e: e.dma_start(out=ident_bf[:, :], in_=ident_d), w=["ident_bf"], dsem=ds_setup_pool)
    S.op("pool", lambda e: e.dma_start(out=masks_bf[:, :, :], in_=masks_d.rearrange("p (m k) -> p m k", m=3)),
         w=["masks"], dsem=ds_setup_pool)
    S.op("pool", lambda e: e.dma_start(out=btm_bf[:, :], in_=btm_d), w=["btm"], dsem=ds_setup_pool)
    S.op("pool", lambda e: e.dma_start(out=bout_bf[:, :], in_=bout_d), w=["bout"], dsem=ds_setup_pool)
    S.op("pool", lambda e: e.dma_start(out=skT_sb[:, :, :], in_=skT_d.rearrange("p (s n) -> p s n", s=2)),
         w=["skT"], dsem=ds_setup_pool)
    wov = w_out_d.rearrange("(c p) n -> p c n", p=128)
    for c in range(8):
        S.op("pool", lambda e, c=c: e.dma_start(out=w_out_sb[:, c, :], in_=wov[:, c, :]),
             w=[("w_out", c)], dsem=ds_w_out)
    wpv = w_pq_d.rearrange("(c p) n -> p c n", p=128)
    for c in range(8):
        S.op("pool", lambda e, c=c: e.dma_start(out=w_pq_sb[:, c, :], in_=wpv[:, c, :]),
             w=[("w_pq", c)], dsem=ds_w_pq)
    PREP_ROWS = 512
    for i in range(NEXP // PREP_ROWS):
        S.op("pool", lambda e, i=i: e.dma_start(out=uvb_d[i * PREP_ROWS:(i + 1) * PREP_ROWS, :],
                                                in_=uv_d[i * PREP_ROWS:(i + 1) * PREP_ROWS, :]),
             w=[("uvb", i)], dsem=ds_prep[i % NPREP],
             xw=([(ds_prep[i % NPREP], 16 * (i // NPREP))] if i >= NPREP else []))
    ALLUVB = [("uvb", i) for i in range(NEXP // PREP_ROWS)]
    ALLW = [("w_in", c, c0) for c in range(8) for c0 in (0, 1216)]
    ALLWO = [("w_out", c) for c in range(8)]
    ALLWP = [("w_pq", c) for c in range(8)]

    S.op("sp", lambda e: e.dma_start(out=cst_sb[:, :], in_=cst_d), w=["cst"], dsem=ds_setup_sp)
    S.op("sp", lambda e: e.dma_start(out=pos_i[:, :], in_=pos_d), w=["pos_i"], dsem=ds_setup_sp)
    S.op("sp", lambda e: e.dma_start(out=bconv_sb[:, :], in_=bconv_d), w=["bconv"], dsem=ds_setup_sp)
    S.op("sp", lambda e: e.dma_start(out=cw_sb[:, :], in_=cw_d), w=["cw"], dsem=ds_setup_sp)
    S.op("sp", lambda e: e.dma_start(out=sinks_sb[:, :], in_=sinks_d.partition_broadcast(128)),
         w=["sinks"], dsem=ds_setup_sp)
    for i in range(4):
        S.op("sp", lambda e, i=i: e.dma_start(out=vecs_sb[:, i, :], in_=vecs_d[i, :].partition_broadcast(128)),
             w=[("vecs", i)], dsem=ds_setup_sp)

    S.op("dve", lambda e: e.memset(ones_bf[:, :], 1.0), w=["ones"])
    S.op("dve", lambda e: e.tensor_copy(out=thr_bf[:, :], in_=cst_sb[:, 8:40]), r=["cst"], w=["thr_bf"])
    S.op("dve", lambda e: e.memset(eps_t[:, :], LN_EPS), w=["eps"])
    for i in range(2):
        S.op("dve", lambda e, i=i: e.memset(v_ext[i][:, :, :], 1.0), w=[("vext", i)])
    S.op("dve", lambda e: e.tensor_copy(out=posf[:, :], in_=pos_i[:, :]), r=["pos_i"], w=["posf"])
    S.op("dve", lambda e: e.tensor_tensor(
        out=ang[:, :, :], in0=posf[:, :].unsqueeze(2).to_broadcast([128, NBLK + 1, 8]),
        in1=invf.unsqueeze(1).to_broadcast([128, NBLK + 1, 8]), op=ALU.mult), r=["posf", "cst"], w=["ang"])
    S.op("dve", lambda e: e.tensor_scalar(out=rtmp[:, :, :], in0=ang[:, :, :], scalar1=1.0 / TWO_PI, scalar2=None,
                                          op0=ALU.mult), r=["ang"], w=["rtmp"])
    S.op("dve", lambda e: e.tensor_copy(out=rki[:, :, :], in_=rtmp[:, :, :]), r=["rtmp"], w=["rki"])
    S.op("dve", lambda e: e.tensor_copy(out=rtmp[:, :, :], in_=rki[:, :, :]), r=["rki"], w=["rtmp"])
    S.op("dve", lambda e: e.scalar_tensor_tensor(out=rtmp2[:, :, :], in0=rtmp[:, :, :], scalar=-CW1, in1=ang[:, :, :],
                                                 op0=ALU.mult, op1=ALU.add), r=["rtmp", "ang"], w=["rtmp2"])
    S.op("dve", lambda e: e.scalar_tensor_tensor(out=ang[:, :, :], in0=rtmp[:, :, :], scalar=-CW2, in1=rtmp2[:, :, :],
                                                 op0=ALU.mult, op1=ALU.add), r=["rtmp", "rtmp2"], w=["ang"])

    def wrap(buf_key, buf):
        S.op("dve", lambda e: e.tensor_single_scalar(out=rtmp[:, :, :], in_=buf[:, :, :], scalar=np.pi, op=ALU.is_gt),
             r=[buf_key], w=["rtmp"])
        S.op("dve", lambda e: e.scalar_tensor_tensor(out=buf[:, :, :], in0=rtmp[:, :, :], scalar=-TWO_PI, in1=buf[:, :, :],
                                                     op0=ALU.mult, op1=ALU.add), r=["rtmp", buf_key], w=[buf_key])
        S.op("dve", lambda e: e.tensor_single_scalar(out=rtmp[:, :, :], in_=buf[:, :, :], scalar=-np.pi, op=ALU.is_lt),
             r=[buf_key], w=["rtmp"])
        S.op("dve", lambda e: e.scalar_tensor_tensor(out=buf[:, :, :], in0=rtmp[:, :, :], scalar=TWO_PI, in1=buf[:, :, :],
                                                     op0=ALU.mult, op1=ALU.add), r=["rtmp", buf_key], w=[buf_key])
        S.op("dve", lambda e: e.tensor_scalar(out=buf[:, :, :], in0=buf[:, :, :], scalar1=PI_SAFE, scalar2=-PI_SAFE,
                                              op0=ALU.min, op1=ALU.max), r=[buf_key], w=[buf_key])

    wrap("ang", ang)
    S.op("act", lambda e: e.activation(out=sin_t[:, :, :], in_=ang[:, :, :], func=AF.Sin), r=["ang"], w=["sin_t"])
    S.op("dve", lambda e: e.tensor_scalar(out=rtmp2[:, :, :], in0=ang[:, :, :], scalar1=float(np.pi / 2), scalar2=None,
                                          op0=ALU.add), r=["ang"], w=["rtmp2"])
    wrap("rtmp2", rtmp2)
    S.op("act", lambda e: e.activation(out=cos_t[:, :, :], in_=rtmp2[:, :, :], func=AF.Sin), r=["rtmp2"], w=["cos_t"])
    S.op("act", lambda e: e.activation(out=esink[:, :], in_=sinks_sb[:, :], func=AF.Exp), r=["sinks"], w=["esink"])

    def phase_a(b):
        L = S.begin()
        xs = 0
        kv = b % 2
        kvp = (b - 1) % 2
        hs = b % 2
        S.op("sp", lambda e: e.dma_start(out=xf[xs][:, :], in_=xh[b * 128:(b + 1) * 128, :]), w=[("xf", xs)], dsem=ds_x[xs])
        S.op("act", lambda e: e.activation(out=xb[:, :], in_=xf[xs][:, :], func=AF.Copy), r=[("xf", xs)], w=["h1b"])
        for c in range(8):
            S.op("pe", lambda e, c=c: e.transpose(out=tp_bf[:, c * 128:(c + 1) * 128], in_=xb[:, c * 128:(c + 1) * 128],
                                                  identity=ident_bf[:, :]), r=["h1b", "ident_bf"], w=[("ps", 0)])
        S.op("act", lambda e: e.activation(out=xT[:, :, :], in_=tp_bf[:, :].rearrange("p (c t) -> p c t", c=8), func=AF.Copy),
             r=[("ps", 0)], w=["xT"])
        groups = []
        if b >= 1:
            groups.append((1, 0, 512))
        groups.append((2, 512, 384))
        for (bk, c0, n) in groups:
            for c in range(8):
                S.op("pe", lambda e, c=c, bk=bk, c0=c0, n=n: e.matmul(
                    out=bank(bk, 0, n), lhsT=xT[:, c, :], rhs=w_in_sb[:, c, c0:c0 + n], start=(c == 0), stop=False),
                    r=["xT"] + ALLW, w=[("ps", bk)])
            S.op("pe", lambda e, bk=bk, c0=c0, n=n: e.matmul(
                out=bank(bk, 0, n), lhsT=ones_bf[0:1, :], rhs=btm_bf[0:1, c0:c0 + n], start=False, stop=True),
                r=["ones", "btm"], w=[("ps", bk)])
        kinds = (1, 2) if b == 0 else (0, 1, 2)
        for kind in kinds:
            bk = 3 + kind
            for cc in range(4):
                col = 896 + kind * 512 + cc * 128
                for c in range(8):
                    S.op("pe", lambda e, c=c, bk=bk, cc=cc, col=col: e.matmul(
                        out=bank(bk, cc * 128, 128), lhsT=w_in_sb[:, c, col:col + 128], rhs=xT[:, c, :],
                        start=(c == 0), stop=(c == 7)), r=["xT"] + ALLW, w=[("ps", bk)])
        if b >= 1:
            S.op("act", lambda e: e.activation(out=qk_sb[:, 0:512], in_=bank(1), func=AF.Copy), r=[("ps", 1)], w=["qk_q"])
        S.op("act", lambda e: e.activation(out=qk_sb[:, 512:768], in_=bank(2, 0, 256), func=AF.Copy), r=[("ps", 2)], w=["qk_k"])
        S.op("act", lambda e: e.activation(out=v_ext[kv][:, :, 0:64], in_=bank(2, 256, 128).rearrange("p (g d) -> p g d", g=2),
                                           func=AF.Copy), r=[("ps", 2)], w=[("vext", kv)])
        h0 = 0 if b >= 1 else 8
        nh = 12 - h0
        qk3 = qk_sb[:, :].rearrange("p (h d) -> p h d", d=64)
        qr3 = qk_rot[:, :].rearrange("p (h d) -> p h d", d=64)
        cosb = cos_t[:, b, :].unsqueeze(1).to_broadcast([128, nh, 8])
        sinb = sin_t[:, b, :].unsqueeze(1).to_broadcast([128, nh, 8])
        r1 = qk3[:, h0:12, 0:8]
        r2_ = qk3[:, h0:12, 8:16]
        qkr = ["qk_q", "qk_k"]
        S.op("act", lambda e: e.activation(out=qk_rot[:, h0 * 64:768], in_=qk_sb[:, h0 * 64:768], func=AF.Copy), r=qkr, w=["qk_rot"])
        S.op("dve", lambda e: e.tensor_tensor(out=rp[0][:, h0:12, :], in0=r1, in1=cosb, op=ALU.mult), r=qkr + ["cos_t"], w=["rp0"])
        S.op("dve", lambda e: e.tensor_tensor(out=rp[1][:, h0:12, :], in0=r2_, in1=sinb, op=ALU.mult), r=qkr + ["sin_t"], w=["rp1"])
        S.op("dve", lambda e: e.tensor_tensor(out=rp[2][:, h0:12, :], in0=r2_, in1=cosb, op=ALU.mult), r=qkr + ["cos_t"], w=["rp2"])
        S.op("dve", lambda e: e.tensor_tensor(out=rp[3][:, h0:12, :], in0=r1, in1=sinb, op=ALU.mult), r=qkr + ["sin_t"], w=["rp3"])
        S.op("dve", lambda e: e.tensor_tensor(out=qr3[:, h0:12, 0:8], in0=rp[0][:, h0:12, :], in1=rp[1][:, h0:12, :], op=ALU.subtract),
             r=["rp0", "rp1"], w=["qk_rot"])
        S.op("dve", lambda e: e.tensor_tensor(out=qr3[:, h0:12, 8:16], in0=rp[2][:, h0:12, :], in1=rp[3][:, h0:12, :], op=ALU.add),
             r=["rp2", "rp3"], w=["qk_rot"])
        jlist = list(range(4, 6)) if b == 0 else list(range(6))
        for j in jlist:
            S.op("pe", lambda e, j=j: e.transpose(out=tp_bf[:, j * 128:(j + 1) * 128], in_=qk_rot[:, j * 128:(j + 1) * 128],
                                                  identity=ident_bf[:, :]), r=["qk_rot", "ident_bf"], w=[("ps", 0)])
        if b >= 1:
            S.op("act", lambda e: e.activation(out=qT[:, :, :], in_=tp_bf[:, 0:512].rearrange("p (c t) -> p c t", c=4), func=AF.Copy),
                 r=[("ps", 0)], w=["qT"])
        S.op("act", lambda e: e.activation(out=kT[kv][:, :, :], in_=tp_bf[:, 512:768].rearrange("p (c t) -> p c t", c=2), func=AF.Copy),
             r=[("ps", 0)], w=[("kT", kv)])

        for cc in range(4):
            S.op("act", lambda e, cc=cc: e.activation(out=gc_sb[:, cc % 2, :], in_=bank(4, cc * 128, 128), func=AF.Identity,
                                                      bias=bconv_sb[:, 4 + cc:5 + cc], scale=1.0),
                 r=[("ps", 4), "bconv"], w=[("gc", cc % 2)])
            if b >= 1:
                S.op("dve", lambda e, cc=cc: e.tensor_copy(out=ubuf[cc][:, 0:2], in_=ubuf[cc][:, 128:130]),
                     r=[("u", cc)], w=[("uh", cc)])
            S.op("dve", lambda e, cc=cc: e.scalar_tensor_tensor(
                out=ubuf[cc][:, 2:130], in0=bank(5, cc * 128, 128), scalar=bconv_sb[:, 8 + cc:9 + cc], in1=gc_sb[:, cc % 2, :],
                op0=ALU.add, op1=ALU.mult), r=[("ps", 5), ("gc", cc % 2), "bconv", ("uh", cc)], w=[("u", cc)])
            if b == 0:
                S.op("dve", lambda e, cc=cc: e.tensor_scalar(out=ubuf[cc][:, 2:130], in0=ubuf[cc][:, 2:130], scalar1=flag,
                                                             scalar2=None, op0=ALU.mult), r=[("u", cc), "cst"], w=[("u", cc)])
            else:
                S.op("act", lambda e, cc=cc: e.activation(out=gb_sb[:, cc % 2, :], in_=bank(3, cc * 128, 128), func=AF.Identity,
                                                          bias=bconv_sb[:, cc:cc + 1], scale=1.0),
                     r=[("ps", 3), "bconv"], w=[("gb", cc % 2)])
                S.op("dve", lambda e, cc=cc: e.tensor_scalar(out=cacc[:, cc % 2, :], in0=ubuf[cc][:, 0:128],
                                                             scalar1=cw_sb[:, cc * 3:cc * 3 + 1], scalar2=None, op0=ALU.mult),
                     r=[("u", cc), ("uh", cc), "cw"], w=[("cacc", cc % 2)])
                S.op("dve", lambda e, cc=cc: e.scalar_tensor_tensor(
                    out=cacc[:, cc % 2, :], in0=ubuf[cc][:, 1:129], scalar=cw_sb[:, cc * 3 + 1:cc * 3 + 2], in1=cacc[:, cc % 2, :],
                    op0=ALU.mult, op1=ALU.add), r=[("u", cc), ("uh", cc), ("cacc", cc % 2)], w=[("cacc", cc % 2)])
                S.op("dve", lambda e, cc=cc: e.scalar_tensor_tensor(
                    out=cacc[:, cc % 2, :], in0=ubuf[cc][:, 2:130], scalar=cw_sb[:, cc * 3 + 2:cc * 3 + 3], in1=cacc[:, cc % 2, :],
                    op0=ALU.mult, op1=ALU.add), r=[("u", cc), ("cacc", cc % 2)], w=[("cacc", cc % 2)])
                S.op("dve", lambda e, cc=cc: e.tensor_tensor(out=yT[:, 4 + cc, :], in0=cacc[:, cc % 2, :], in1=gb_sb[:, cc % 2, :], op=ALU.mult),
                     r=[("cacc", cc % 2), ("gb", cc % 2)], w=[("yT", 4 + cc)])
        if b == 0:
            return L

        mP = 2 if b == 1 else 0
        for h in range(8):
            g, j, base = h // 4, h // 2, (h % 2) * 64
            for (which, bk, msk, kslot) in ((0, 1 + 2 * g, mP, kvp), (1, 2 + 2 * g, 1, kv)):
                reg = bank(bk, (h % 4) * 128, 128)
                S.op("pe", lambda e, reg=reg, msk=msk: e.matmul(out=reg, lhsT=ident_bf[:, :], rhs=masks_bf[:, msk, :],
                                                                start=True, stop=False), r=["ident_bf", "masks"], w=[("ps", bk)])
                S.op("pe", lambda e, reg=reg, kslot=kslot, g=g, j=j, base=base: e.matmul(
                    out=reg, lhsT=kT[kslot][base:base + 64, g, :], rhs=qT[base:base + 64, j, :], start=False, stop=True),
                    r=[("kT", kslot), "qT"], w=[("ps", bk)])
        for bk in range(1, 5):
            S.op("act", lambda e, bk=bk: e.activation(out=pT[:, (bk - 1) * 4:bk * 4, :],
                                                      in_=bank(bk).rearrange("p (h t) -> p h t", h=4), func=AF.Exp, scale=0.125),
                 r=[("ps", bk)], w=[("pT", bk)])
        for g in range(2):
            for hh in range(4):
                reg = bank(5, hh * 65, 65)
                S.op("pe", lambda e, reg=reg, g=g, hh=hh: e.matmul(out=reg, lhsT=pT[:, (2 * g) * 4 + hh, :],
                                                                   rhs=v_ext[kvp][:, g, 0:65], start=True, stop=False),
                     r=[("pT", 1 + 2 * g), ("vext", kvp)], w=[("ps", 5)])
                S.op("pe", lambda e, reg=reg, g=g, hh=hh: e.matmul(out=reg, lhsT=pT[:, (2 * g + 1) * 4 + hh, :],
                                                                   rhs=v_ext[kv][:, g, 0:65], start=False, stop=True),
                     r=[("pT", 2 + 2 * g), ("vext", kv)], w=[("ps", 5)])
            o3 = bank(5, 0, 260).rearrange("p (h d) -> p h d", d=65)
            S.op("dve", lambda e, g=g, o3=o3: e.tensor_tensor(out=den_sb[:, :], in0=o3[:, :, 64], in1=esink[:, g * 4:(g + 1) * 4], op=ALU.add),
                 r=[("ps", 5), "esink"], w=["den"])
            S.op("dve", lambda e: e.reciprocal(out=rden[:, :], in_=den_sb[:, :]), r=["den"], w=["rden"])
            S.op("dve", lambda e, g=g, o3=o3: e.tensor_tensor(out=yatt[:, g * 4:(g + 1) * 4, :], in0=o3[:, :, 0:64],
                                                              in1=rden[:, :].unsqueeze(2).to_broadcast([128, 4, 64]), op=ALU.mult),
                 r=[("ps", 5), "rden"], w=[("yatt", g)])
        yatt2 = yatt[:, :, :].rearrange("p h d -> p (h d)")
        for j in range(4):
            S.op("pe", lambda e, j=j: e.transpose(out=tp_bf[:, j * 128:(j + 1) * 128], in_=yatt2[:, j * 128:(j + 1) * 128],
                                                  identity=ident_bf[:, :]), r=[("yatt", 0), ("yatt", 1), "ident_bf"], w=[("ps", 0)])
        S.op("act", lambda e: e.activation(out=yT[:, 0:4, :], in_=tp_bf[:, 0:512].rearrange("p (c t) -> p c t", c=4), func=AF.Copy),
             r=[("ps", 0)], w=[("yT", c) for c in range(4)])
        for half in range(2):
            bk = 1 + half
            for c in range(8):
                S.op("pe", lambda e, c=c, bk=bk, half=half: e.matmul(out=bank(bk), lhsT=yT[:, c, :],
                                                                     rhs=w_out_sb[:, c, half * 512:(half + 1) * 512],
                                                                     start=(c == 0), stop=False),
                     r=[("yT", c)] + ALLWO, w=[("ps", bk)])
            S.op("pe", lambda e, bk=bk, half=half: e.matmul(out=bank(bk), lhsT=ones_bf[0:1, :],
                                                            rhs=bout_bf[0:1, half * 512:(half + 1) * 512], start=False, stop=True),
                 r=["ones", "bout"], w=[("ps", bk)])
        rbuf = xf[xs]
        S.op("dve", lambda e: e.scalar_tensor_tensor(out=rbuf[:, :], in0=xf[xs][:, :], scalar=ALPHA, in1=ps[:, 512:1536],
                                                     op0=ALU.mult, op1=ALU.add), r=[("xf", xs), ("ps", 1), ("ps", 2)], w=[("xf", xs)])
        layer_norm(S, "1", rbuf, ("xf", xs), st1, mv1, sd1, rstd1, nmr1, vecs_sb[:, 0, :], vecs_sb[:, 1, :], [("vecs", 0), ("vecs", 1)],
                   h1[hs], ("h1", hs))
        S.op("act", lambda e: e.activation(out=h1b1[:, :], in_=h1[hs][:, :], func=AF.Copy), r=[("h1", hs)], w=["h1b"])
        for c in range(8):
            S.op("pe", lambda e, c=c: e.transpose(out=tp_bf[:, c * 128:(c + 1) * 128], in_=h1b1[:, c * 128:(c + 1) * 128],
                                                  identity=ident_bf[:, :]), r=["h1b", "ident_bf"], w=[("ps", 0)])
        S.op("act", lambda e: e.activation(out=h1T[:, :, :], in_=tp_bf[:, :].rearrange("p (c t) -> p c t", c=8), func=AF.Copy),
             r=[("ps", 0)], w=["xT"])
        for jj in range(16):
            bk = 1 + jj // 4
            for c in range(8):
                S.op("pe", lambda e, c=c, jj=jj, bk=bk: e.matmul(out=bank(bk, (jj % 4) * 128, 128),
                                                                 lhsT=w_pq_sb[:, c, jj * 128:(jj + 1) * 128], rhs=h1T[:, c, :],
                                                                 start=(c == 0), stop=(c == 7)),
                     r=["xT"] + ALLWP, w=[("ps", bk)])
        for bk in range(1, 5):
            S.op("act", lambda e, bk=bk: e.activation(out=qpT[:, (bk - 1) * 4:bk * 4, :],
                                                      in_=bank(bk).rearrange("p (h t) -> p h t", h=4), func=AF.Copy),
                 r=[("ps", bk)], w=[("pT", bk)])
        for jj in range(16):
            bk = 1 + jj // 4
            S.op("pe", lambda e, jj=jj, bk=bk: e.matmul(out=bank(bk, (jj % 4) * 128, 128), lhsT=qpT[:, jj, :],
                                                        rhs=skT_sb[:, jj % 2, :], start=True, stop=True),
                 r=[("pT", bk), "skT"], w=[("ps", bk)])

        def top16(src_fn, key, vdst, idst, vkey, ikey):
            S.op("dve", lambda e: e.max(out=vdst[:, 0:8], in_=src_fn()), r=[key], w=[vkey])
            S.op("dve", lambda e: e.max_index(out=idst[:, 0:8], in_max=vdst[:, 0:8], in_values=src_fn()), r=[key, vkey], w=[ikey])
            S.op("dve", lambda e: e.match_replace(out=src_fn(), in_to_replace=vdst[:, 0:8], in_values=src_fn(), imm_value=-1e30),
                 r=[vkey, ikey, key], w=[key])
            S.op("dve", lambda e: e.max(out=vdst[:, 8:16], in_=src_fn()), r=[key], w=[vkey])
            S.op("dve", lambda e: e.max_index(out=idst[:, 8:16], in_max=vdst[:, 8:16], in_values=src_fn()), r=[key, vkey], w=[ikey])

        for jj in range(16):
            top16(lambda jj=jj: bank(1 + jj // 4, (jj % 4) * 128, 128), ("ps", 1 + jj // 4), vA[:, jj, :], iA[:, jj, :], ("vA", jj), ("iA", jj))
        ALLS = [("ps", bk) for bk in range(1, 5)]
        ALLVA = [("vA", jj) for jj in range(16)]
        ALLIA = [("iA", jj) for jj in range(16)]
        vA4 = vA[:, :, :].rearrange("p (h s) k -> p h s k", s=2)
        iA4 = iA[:, :, :].rearrange("p (h s) k -> p h s k", s=2)
        cand = ps[:, 512:2560].rearrange("p (h c) -> p h c", c=256)
        cand4 = cand.rearrange("p h (a b) -> p h a b", b=16)
        S.op("dve", lambda e: e.tensor_tensor(out=cand4, in0=vA4[:, :, 0, :].unsqueeze(3).to_broadcast([128, 8, 16, 16]),
                                              in1=vA4[:, :, 1, :].unsqueeze(2).to_broadcast([128, 8, 16, 16]), op=ALU.add),
             r=ALLVA + ALLS, w=ALLS)
        for h in range(8):
            top16(lambda h=h: cand[:, h, :], ("ps", 1 + h // 2), sc[:, h, :], posu[:, h, :], ("sc", h), ("posu", h))
        ALLSC = [("sc", h) for h in range(8)]
        ALLPOS = [("posu", h) for h in range(8)]
        S.op("dve", lambda e: e.tensor_copy(out=pposf[:, :], in_=posu[:, :, :].rearrange("p h k -> p (h k)")), r=ALLPOS, w=["pposf"])
        S.op("dve", lambda e: e.tensor_copy(out=i12f[:, :, :], in_=iA[:, :, :]), r=ALLIA, w=["i12f"])
        S.op("dve", lambda e: e.tensor_copy(out=d12[:, :, 0:1], in_=i12f[:, :, 0:1]), r=["i12f"], w=["d12a"])
        S.op("dve", lambda e: e.tensor_tensor(out=d12[:, :, 1:16], in0=i12f[:, :, 1:16], in1=i12f[:, :, 0:15], op=ALU.subtract),
             r=["i12f"], w=["d12b"])
        d4 = d12[:, :, :].rearrange("p (h s) k -> p h s k", s=2)
        for half in range(2):
            hk0 = half * 64
            age4 = age[:, :, :].rearrange("p (h k) a -> p h k a", h=4)
            prod4 = prod[:, :, :].rearrange("p (h k) a -> p h k a", h=4)
            S.op("dve", lambda e, hk0=hk0: e.tensor_tensor(
                out=age[:, :, :], in0=pposf[:, hk0:hk0 + 64].unsqueeze(2).to_broadcast([128, 64, 16]),
                in1=thrA.unsqueeze(1).to_broadcast([128, 64, 16]), op=ALU.is_ge), r=["pposf", "thr_bf"], w=["age", ("pT", 1), ("pT", 2)])
            S.op("dve", lambda e: e.tensor_reduce(out=sumA[:, :], in_=age[:, :, :], axis=AX.X, op=ALU.add), r=["age"], w=["sumA"])
            S.op("dve", lambda e, half=half, age4=age4, prod4=prod4: e.tensor_tensor(
                out=prod4, in0=age4, in1=d4[:, half * 4:(half + 1) * 4, 0, :].unsqueeze(2).to_broadcast([128, 4, 16, 16]),
                op=ALU.mult), r=["age", "d12a", "d12b"], w=["prod", ("pT", 3), ("pT", 4)])
            S.op("dve", lambda e, hk0=hk0: e.tensor_reduce(out=isel[:, 0, hk0:hk0 + 64], in_=prod[:, :, :], axis=AX.X, op=ALU.add),
                 r=["prod"], w=[("isel", 0, half)])
            S.op("dve", lambda e, hk0=hk0: e.scalar_tensor_tensor(out=bq[:, :], in0=sumA[:, :], scalar=-16.0, in1=pposf[:, hk0:hk0 + 64],
                                                                   op0=ALU.mult, op1=ALU.add), r=["sumA", "pposf"], w=["bq"])
            S.op("dve", lambda e: e.tensor_tensor(
                out=age[:, :, :], in0=bq[:, :].unsqueeze(2).to_broadcast([128, 64, 16]),
                in1=thrB.unsqueeze(1).to_broadcast([128, 64, 16]), op=ALU.is_ge), r=["bq", "thr_bf", "prod"], w=["age", ("pT", 1), ("pT", 2)])
            S.op("dve", lambda e, half=half, age4=age4, prod4=prod4: e.tensor_tensor(
                out=prod4, in0=age4, in1=d4[:, half * 4:(half + 1) * 4, 1, :].unsqueeze(2).to_broadcast([128, 4, 16, 16]),
                op=ALU.mult), r=["age", "d12a", "d12b"], w=["prod", ("pT", 3), ("pT", 4)])
            S.op("dve", lambda e, hk0=hk0: e.tensor_reduce(out=isel[:, 1, hk0:hk0 + 64], in_=prod[:, :, :], axis=AX.X, op=ALU.add),
                 r=["prod"], w=[("isel", 1, half)])
            S.op("dve", lambda e, hk0=hk0: e.scalar_tensor_tensor(out=ef[:, hk0:hk0 + 64], in0=isel[:, 0, hk0:hk0 + 64], scalar=128.0,
                                                                   in1=isel[:, 1, hk0:hk0 + 64], op0=ALU.mult, op1=ALU.add),
                 r=[("isel", 0, half), ("isel", 1, half)], w=[("ef", half)])
            S.op("dve", lambda e, hk0=hk0: e.tensor_copy(out=e_i32[b % 2][:, hk0:hk0 + 64], in_=ef[:, hk0:hk0 + 64]),
                 r=[("ef", half)], w=[("e_i32", b % 2, half)])
        es_ = b % 2
        S.op("dve", lambda e: e.tensor_tensor(out=scs[:, :, :], in0=sc[:, :, :], in1=sc[:, :, 0:1].to_broadcast([128, 8, 16]),
                                              op=ALU.subtract), r=ALLSC, w=["scs"])
        S.op("act", lambda e: e.activation(out=pex[:, :, :], in_=scs[:, :, :], func=AF.Exp), r=["scs"], w=["pex"])
        S.op("dve", lambda e: e.tensor_reduce(out=ssum[:, :], in_=pex[:, :, :], axis=AX.X, op=ALU.add), r=["pex"], w=["ssum"])
        S.op("dve", lambda e: e.reciprocal(out=rsum[:, :], in_=ssum[:, :]), r=["ssum"], w=["rsum"])
        S.op("dve", lambda e: e.tensor_tensor(out=gw[es_][:, :, :], in0=pex[:, :, :],
                                              in1=rsum[:, :].unsqueeze(2).to_broadcast([128, 8, 16]), op=ALU.mult),
             r=["pex", "rsum"], w=[("gw", es_)])
        return L

    cnt = {"gu": 0, "dg": 0}
    LDEF = {}

    def phase_b(b):
        L = S.begin()
        hs = b % 2
        es_ = b % 2
        gw2 = gw[es_][:, :, :].rearrange("p h k -> p (h k)")
        ring = {}

        def st_gather(s):
            r = cnt["gu"] % NG
            cnt["gu"] += 1
            ring[s] = r
            S.op("pool", lambda e, s=s, r=r: e.indirect_dma_start(
                out=guv[r][:, :], out_offset=None, in_=uvb_d,
                in_offset=bass.IndirectOffsetOnAxis(ap=e_i32[es_][:, s:s + 1], axis=0)),
                r=[("e_i32", es_, s // 64)] + ALLUVB, w=[("gu", r), ("gv", r)], dsem=ds_g[r])

        def st_dot(s):
            r = ring[s]
            S.op("dve", lambda e, s=s, r=r: e.scalar_tensor_tensor(
                out=guv[r][:, 0:D], in0=guv[r][:, 0:D], scalar=1.0, in1=h1[hs][:, :], op0=ALU.mult, op1=ALU.mult,
                accum_out=a_sb[:, s:s + 1]), r=[("gu", r), ("h1", hs)], w=[("gu", r), ("a", s)])

        def st_gelu(s):
            S.op("act", lambda e, s=s: e.activation(out=ga_sb[:, s:s + 1], in_=a_sb[:, s:s + 1], func=AF.Gelu),
                 r=[("a", s)], w=[("ga", s)])

        def st_coef(s):
            S.op("act", lambda e, s=s: e.activation(out=coef[:, s:s + 1], in_=ga_sb[:, s:s + 1], func=AF.Identity,
                                                    scale=gw2[:, s:s + 1]),
                 r=[("ga", s), ("gw", es_)], w=[("coef", s)])

        def st_diag(s):
            rd = cnt["dg"] % NDR
            cnt["dg"] += 1
            ring[("d", s)] = rd
            S.op("act", lambda e, s=s, rd=rd: e.activation(out=dg[rd][:, :], in_=ident_bf[:, :], func=AF.Identity,
                                                           scale=coef[:, s:s + 1]),
                 r=[("coef", s), "ident_bf"], w=[("dg", rd)])

        def st_mm(s):
            r = ring[s]
            rd = ring[("d", s)]
            for half in range(2):
                S.op("pe", lambda e, s=s, rd=rd, r=r, half=half: e.matmul(
                    out=bank(6 + half), lhsT=dg[rd][:, :], rhs=guv[r][:, D + half * 512:D + (half + 1) * 512],
                    start=(s == 0), stop=(s == 127)), r=[("dg", rd), ("gv", r)], w=[("ps", 6 + half)])

        for s in range(128 + 2):
            if s < 128:
                st_gather(s)
                st_dot(s)
            if 0 <= s - 1 < 128:
                st_gelu(s - 1)
            if 0 <= s - 2 < 128:
                st_diag(s - 2)
                st_mm(s - 2)
            if 0 <= s - 1 < 128:
                st_coef(s - 1)
        if any(t[0] == "ff" for t in taps):
            S.op("dve", lambda e: e.tensor_copy(out=h1[1 - hs][:, :], in_=ps[:, 3072:4096]), r=[("ps", 6), ("ps", 7)], w=[("h1", 1 - hs)])
        S.op("dve", lambda e: e.scalar_tensor_tensor(out=r2[:, :], in0=h1[hs][:, :], scalar=ALPHA, in1=ps[:, 3072:4096],
                                                     op0=ALU.mult, op1=ALU.add), r=[("h1", hs), ("ps", 6), ("ps", 7)], w=["r2"])
        ldef = S.begin()
        layer_norm(S, "2", r2, "r2", st2, mv2, sd2, rstd2, nmr2, vecs_sb[:, 2, :], vecs_sb[:, 3, :], [("vecs", 2), ("vecs", 3)],
                   r2, "r2")
        S.op("sp", lambda e: e.dma_start(out=y_d[(b - 1) * 128:b * 128, :], in_=r2[:, :]), r=["r2"], dsem=ds_out)
        S.cur = L
        LDEF[b] = ldef
        return L

    def layer_norm(S, tag, buf, bkey, st, mv, sd, rstd, nmr, g_ap, b_ap, vkeys, dst, dkey):
        S.op("dve", lambda e: e.bn_stats(out=st[:, 0:6], in_=buf[:, 0:512]), r=[bkey], w=["st0" + tag])
        S.op("dve", lambda e: e.bn_stats(out=st[:, 6:12], in_=buf[:, 512:1024]), r=[bkey], w=["st1" + tag])
        S.op("dve", lambda e: e.bn_aggr(out=mv[:, :], in_=st[:, :]), r=["st0" + tag, "st1" + tag], w=["mv" + tag])
        S.op("act", lambda e: e.activation(out=sd[:, :], in_=mv[:, 1:2], func=AF.Sqrt, bias=eps_t[:, 0:1], scale=1.0),
             r=["mv" + tag, "eps"], w=["sd" + tag])
        S.op("dve", lambda e: e.reciprocal(out=rstd[:, :], in_=sd[:, :]), r=["sd" + tag], w=["rstd" + tag])
        S.op("dve", lambda e: e.tensor_scalar(out=nmr[:, :], in0=mv[:, 0:1], scalar1=rstd[:, 0:1], scalar2=-1.0,
                                              op0=ALU.mult, op1=ALU.mult), r=["mv" + tag, "rstd" + tag], w=["nmr" + tag])
        S.op("act", lambda e: e.activation(out=buf[:, :], in_=buf[:, :], func=AF.Identity, bias=nmr[:, 0:1], scale=rstd[:, 0:1]),
             r=[bkey, "nmr" + tag, "rstd" + tag], w=[bkey])
        S.op("dve", lambda e: e.tensor_tensor(out=buf[:, :], in0=buf[:, :], in1=g_ap, op=ALU.mult), r=[bkey, vkeys[0]], w=[bkey])
        S.op("dve", lambda e: e.tensor_tensor(out=dst[:, :], in0=buf[:, :], in1=b_ap, op=ALU.add), r=[bkey, vkeys[1]], w=[dkey])

    seq = list(L_setup)
    LA = [phase_a(b) for b in range(nblk + 1)]
    LB = {b: phase_b(b) for b in range(1, nblk + 1)}
    seq += LA[0]
    seq += LA[1]
    def with_deferred(lst, ldef, start=0.04, step=0.02):
        out = list(lst)
        n = len(lst)
        for i, o in enumerate(ldef):
            out.insert(min(len(out), int(n * (start + step * i)) + i), o)
        return out

    if nblk >= 2:
        seq += LA[2]
        seq += LB[1]
        for b in range(3, nblk + 1):
            seq += with_deferred(merge_lists(LA[b], LB[b - 1]), LDEF[b - 2])
        seq += with_deferred(LB[nblk], LDEF[nblk - 1])
    else:
        seq += LB[nblk]
    seq += LDEF[nblk]
    tap_src = {"h1": (h1[nblk % 2], ("h1", nblk % 2)), "e": (e_i32[nblk % 2], ("e_i32", nblk % 2, 1)),
               "gw": (gw[nblk % 2], ("gw", nblk % 2)), "qk_rot": (qk_rot, "qk_rot"), "yT": (yT, ("yT", 0)),
               "a": (a_sb, ("a", 127)), "coef": (coef, ("coef", 127)), "vA": (vA, ("vA", 0)),
               "iA": (iA, ("iA", 0)), "sc": (sc, ("sc", 0)), "posu": (posu, ("posu", 0)), "cos": (cos_t, "cos_t"),
               "sin": (sin_t, "sin_t"), "qpT": (qpT, ("pT", 1)), "r2": (r2, "r2"), "isel": (isel, ("isel", 0, 0)), "ff": (h1[(nblk + 1) % 2], ("h1", (nblk + 1) % 2))}
    S.begin()
    for (tname, shape, dt) in taps:
        t_ap, key = tap_src[tname]
        dst = tap_d[tname]
        src = t_ap[:, :] if len(t_ap.shape) == 2 else t_ap[:, :, :].rearrange("p a b -> p (a b)")
        S.op("sp", lambda e, dst=dst, src=src: e.dma_start(out=dst, in_=src), r=[key], dsem=ds_tap)
    seq += S.cur

    resolve(seq, eng_sems)

    handles = {"pe": "tensor", "act": "scalar", "dve": "vector", "pool": "gpsimd", "sp": "sync"}
    with nc.Block() as block:
        def emit(engname):
            def body(e):
                waited = {}
                for o in seq:
                    if o.eng != engname:
                        continue
                    for (sem, val, grp) in o.waits:
                        if grp is not None:
                            val = grp.count
                        k = id(sem)
                        if waited.get(k, 0) < val:
                            e.wait_ge(sem, val)
                            waited[k] = val
                    inst = o.fn(e)
                    if o.dsem is not None:
                        inst.then_inc(o.dsem.sem, 16)
                    else:
                        inst.then_inc(eng_sems[engname], 1)
                if engname == "sp":
                    for dsm in (ds_out, ds_tap):
                        if dsm.count:
                            e.wait_ge(dsm.sem, dsm.count)
            return body
        block.tensor(emit("pe"))
        block.scalar(emit("act"))
        block.vector(emit("dve"))
        block.gpsimd(emit("pool"))
        block.sync(emit("sp"))
    es.close()
    return nc


_PROG_CACHE = {}


def prep_inputs(inputs, nblk=NBLK):
    x = np.asarray(inputs["x"], dtype=np.float32)
    positions = np.asarray(inputs["positions"], dtype=np.int32)
    w_in = np.asarray(inputs["w_in"], dtype=np.float32)[0]
    b_in = np.asarray(inputs["b_in"], dtype=np.float32)[0]
    cols = np.concatenate([np.arange(0, 512), np.arange(512, 576), np.arange(512, 576), np.arange(576, 640),
                           np.arange(576, 640), np.arange(640, 768), np.arange(768, 2304)])
    w_in_r = np.ascontiguousarray(w_in[:, cols])
    b_in_r = b_in[cols]
    b_tm = np.ascontiguousarray(b_in_r[None, 0:896])
    b_conv = np.ascontiguousarray(b_in_r[896:].reshape(12, 128).T)
    conv_w = np.asarray(inputs["conv_w"], dtype=np.float32)[0]
    cw = np.ascontiguousarray(conv_w.reshape(3, 4, 128).transpose(2, 1, 0).reshape(128, 12))
    sinks = np.ascontiguousarray(np.asarray(inputs["attn_sinks"], dtype=np.float32)[0])
    w_out = np.ascontiguousarray(np.asarray(inputs["w_out"], dtype=np.float32)[0])
    b_out = np.ascontiguousarray(np.asarray(inputs["b_out"], dtype=np.float32)[0][None, :])
    vecs = np.ascontiguousarray(np.stack([np.asarray(inputs[k], dtype=np.float32)[0]
                                          for k in ("ln1_g", "ln1_b", "ln2_g", "ln2_b")]))
    w_pq = np.ascontiguousarray(np.asarray(inputs["w_pq"], dtype=np.float32)[0])
    sk1 = np.asarray(inputs["sub_keys1"], dtype=np.float32)[0]
    sk2 = np.asarray(inputs["sub_keys2"], dtype=np.float32)[0]
    skT = np.ascontiguousarray(np.concatenate([sk1.T, sk2.T], axis=1))
    uv_exp = np.concatenate([np.asarray(inputs["u_experts"], dtype=np.float32)[0],
                             np.asarray(inputs["v_experts"], dtype=np.float32)[0]], axis=1)
    ident = np.eye(128, dtype=np.float32)
    kj = np.arange(128)[:, None]
    qi = np.arange(128)[None, :]
    maskP = np.where(kj > qi, 0.0, NEG).astype(np.float32)
    maskC = np.where(kj <= qi, 0.0, NEG).astype(np.float32)
    invf = (500000.0 ** (-np.arange(0, 16, 2, dtype=np.float32) / 16.0)).astype(np.float32)
    shared = dict(w_in_r=w_in_r, b_tm=b_tm, b_conv=b_conv, cw=cw, sinks=sinks, w_out=w_out, b_out=b_out, vecs=vecs,
                  w_pq=w_pq, skT=skT, uv_exp=uv_exp, ident=ident)
    in_maps = []
    for c in range(NCORES):
        bi, s0 = c // 4, (c % 4) * TPC
        first = (c % 4 == 0)
        nrows = (nblk + 1) * 128
        xh = np.zeros((nrows, D), dtype=np.float32)
        pp = np.zeros((NBLK + 1) * 128, dtype=np.int32)
        if first:
            xh[128:] = x[bi, s0:s0 + nblk * 128]
            pp[128:128 + nblk * 128] = positions[bi, s0:s0 + nblk * 128]
        else:
            xh[:] = x[bi, s0 - 128:s0 + nblk * 128]
            pp[:nrows] = positions[bi, s0 - 128:s0 + nblk * 128]
        pos = np.ascontiguousarray(pp.reshape(NBLK + 1, 128).T)
        maskP0 = np.full((128, 128), NEG, dtype=np.float32) if first else maskP
        masks = np.ascontiguousarray(np.concatenate([maskP, maskC, maskP0], axis=1))
        cst = np.zeros((128, 64), dtype=np.float32)
        cst[:, 0:8] = invf[None, :]
        cst[:, 8:24] = (16.0 * np.arange(16, dtype=np.float32))[None, :]
        cst[:, 24:40] = (np.arange(16, dtype=np.float32) - 16.0)[None, :]
        cst[:, 40] = 0.0 if first else 1.0
        m = dict(shared)
        m.update(xh=xh, pos=pos, masks=masks, cst=cst)
        in_maps.append(m)
    return in_maps


def kernel(**inputs):
    if "nc" not in _PROG_CACHE:
        _PROG_CACHE["nc"] = build_program()
    nc = _PROG_CACHE["nc"]
    in_maps = prep_inputs(inputs)
    res = run_bass_kernel_spmd(nc, in_maps, core_ids=list(range(NCORES)))
    out = np.empty((2, 8192, D), dtype=np.float32)
    for c in range(NCORES):
        bi, s0 = c // 4, (c % 4) * TPC
        out[bi, s0:s0 + TPC] = res.results[c]["y"]
    return out
```

```python
import contextlib
import numpy as np
import concourse.bass as bass
import concourse.mybir as mybir
from concourse.bass_utils import run_bass_kernel_spmd

F32 = mybir.dt.float32
BF16 = mybir.dt.bfloat16
I32 = mybir.dt.int32
U32 = mybir.dt.uint32
ALU = mybir.AluOpType
AF = mybir.ActivationFunctionType
AX = mybir.AxisListType

NCORES = 8
D = 1024
TPC = 2048
NBLK = TPC // 128
WIN_COLS = 2432
NEXP = 16384
ALPHA = float(2.0 ** 0.25)
LN_EPS = 1e-5
NEG = -30000.0
TWO_PI = 6.283185307179586
CW1 = 6.28125
CW2 = TWO_PI - CW1
PI_SAFE = 3.1415925

NG = 10
NDR = 6


class DSem:
    def __init__(self, sem, group=False):
        self.sem = sem
        self.count = 0
        self.group = group


class Op:
    __slots__ = ("eng", "fn", "r", "w", "dsem", "val", "idx", "waits", "name", "xw")

    def __init__(self, eng, fn, r, w, dsem, name):
        self.eng, self.fn, self.r, self.w, self.dsem, self.name = eng, fn, tuple(r), tuple(w), dsem, name
        self.val = None
        self.idx = None
        self.waits = None
        self.xw = ()


class Sched:
    def __init__(self):
        self.cur = None

    def begin(self):
        self.cur = []
        return self.cur

    def op(self, eng, fn, r=(), w=(), dsem=None, name="", xw=()):
        o = Op(eng, fn, r, w, dsem, name)
        o.xw = tuple(xw)
        self.cur.append(o)
        return o


HOP_BONUS = 0.5
ENG_W = {"dve": 1.0, "act": 0.6, "pe": 0.1, "sp": 20.0, "pool": 0.05}


def merge_lists(la, lb, frac=0.97):
    if not lb:
        return list(la)
    if not la:
        return list(lb)
    items = []
    nb = len(lb)
    for j, o in enumerate(lb):
        items.append((j + 0.5, 1, j, o))
    def wof(i):
        o = la[i]
        w = ENG_W[o.eng]
        if i > 0 and o.eng in ("dve", "act") and la[i - 1].eng != o.eng and la[i - 1].eng in ("dve", "act", "pe"):
            w += HOP_BONUS
        return w
    ws = [wof(i) for i in range(len(la))]
    tot = sum(ws)
    acc = 0.0
    for i, o in enumerate(la):
        w = ws[i]
        items.append(((acc + 0.5 * w) / tot * frac * nb, 0, i, o))
        acc += w
    items.sort(key=lambda t: (t[0], t[1], t[2]))
    return [t[3] for t in items]


def resolve(seq, eng_sems):
    last_writer = {}
    readers = {}
    counts = {e: 0 for e in eng_sems}
    for o in seq:
        deps = []
        for k in o.r:
            lw = last_writer.get(k)
            if lw is not None:
                deps.append(lw)
        for k in o.w:
            lw = last_writer.get(k)
            if lw is not None:
                deps.append(lw)
            deps.extend(readers.get(k, ()))
        waits = {}
        for d in deps:
            if d is o:
                continue
            if d.dsem is not None:
                key = id(d.dsem)
                if key not in waits or waits[key][1] < d.val:
                    waits[key] = (d.dsem.sem, d.val, d.dsem if d.dsem.group else None)
            else:
                if d.eng == "pe" and o.eng == "pe":
                    continue
                key = d.eng
                if key not in waits or waits[key][1] < d.idx:
                    waits[key] = (eng_sems[d.eng], d.idx, None)
        o.waits = list(waits.values()) + [(d_.sem, v_, None) for (d_, v_) in o.xw]
        if o.dsem is not None:
            o.dsem.count += 16
            o.val = o.dsem.count
        else:
            counts[o.eng] += 1
            o.idx = counts[o.eng]
        for k in o.r:
            readers.setdefault(k, []).append(o)
        for k in o.w:
            last_writer[k] = o
            readers[k] = []


def build_program(nblk=NBLK, taps=()):
    nc = bass.Bass("TRN2", target_bir_lowering=False)
    es = contextlib.ExitStack()

    def din(name, shape, dt=F32):
        return nc.dram_tensor(name, list(shape), dt, kind="ExternalInput").ap()

    nrows = (nblk + 1) * 128
    xh = din("xh", [nrows, D])
    pos_d = din("pos", [128, NBLK + 1], I32)
    w_in_d = din("w_in_r", [D, WIN_COLS])
    btm_d = din("b_tm", [1, 896])
    bconv_d = din("b_conv", [128, 12])
    cw_d = din("cw", [128, 12])
    sinks_d = din("sinks", [8])
    w_out_d = din("w_out", [D, D])
    bout_d = din("b_out", [1, D])
    vecs_d = din("vecs", [4, D])
    w_pq_d = din("w_pq", [D, 2048])
    skT_d = din("skT", [128, 256])
    uv_d = din("uv_exp", [NEXP, 2 * D])
    uvb_d = nc.dram_tensor("uvb", [NEXP, 2 * D], BF16, kind="Internal").ap()
    ident_d = din("ident", [128, 128])
    masks_d = din("masks", [128, 384])
    cst_d = din("cst", [128, 64])
    y_d = nc.dram_tensor("y", [nblk * 128, D], F32, kind="ExternalOutput").ap()
    tap_d = {}
    for (tname, shape, dt) in taps:
        tap_d[tname] = nc.dram_tensor("tap_" + tname, list(shape), dt, kind="ExternalOutput").ap()

    def sb(name, shape, dt=F32):
        return es.enter_context(nc.sbuf_tensor(name, list(shape), dt))

    w_in_sb = sb("w_in_sb", [128, 8, WIN_COLS], BF16)
    w_out_sb = sb("w_out_sb", [128, 8, D], BF16)
    w_pq_sb = sb("w_pq_sb", [128, 8, 2048], BF16)
    skT_sb = sb("skT_sb", [128, 2, 128], BF16)
    ident_bf = sb("ident_bf", [128, 128], BF16)
    masks_bf = sb("masks_bf", [128, 3, 128], BF16)
    btm_bf = sb("btm_bf", [1, 896], BF16)
    bout_bf = sb("bout_bf", [1, D], BF16)
    ones_bf = sb("ones_bf", [1, 128], BF16)
    vecs_sb = sb("vecs_sb", [128, 4, D], F32)
    bconv_sb = sb("bconv_sb", [128, 12], F32)
    cw_sb = sb("cw_sb", [128, 12], F32)
    sinks_sb = sb("sinks_sb", [128, 8], F32)
    esink = sb("esink", [128, 8], F32)
    cst_sb = sb("cst_sb", [128, 64], F32)
    pos_i = sb("pos_i", [128, NBLK + 1], I32)
    posf = sb("posf", [128, NBLK + 1], F32)
    NB8 = (NBLK + 1) * 8
    sin_t = sb("sin_t", [128, NBLK + 1, 8], F32)
    cos_t = sb("cos_t", [128, NBLK + 1, 8], F32)
    eps_t = sb("eps_t", [128, 1], F32)

    thr_bf = sb("thr_bf", [128, 32], BF16)
    invf = cst_sb[:, 0:8]
    thrA = thr_bf[:, 0:16]
    thrB = thr_bf[:, 16:32]
    flag = cst_sb[:, 40:41]

    guv = [sb(f"guv{i}", [128, 2 * D], BF16) for i in range(NG)]
    _sflat = guv[0][:, :].bitcast(F32)
    ang = _sflat[:, 0:NB8].rearrange("p (b f) -> p b f", f=8)
    rtmp = _sflat[:, 256:256 + NB8].rearrange("p (b f) -> p b f", f=8)
    rtmp2 = _sflat[:, 512:512 + NB8].rearrange("p (b f) -> p b f", f=8)
    rki = _sflat[:, 768:768 + NB8].bitcast(I32).rearrange("p (b f) -> p b f", f=8)
    xf = [sb(f"xf{i}", [128, D], F32) for i in range(1)]
    xT = sb("xT", [128, 8, 128], BF16)
    qk_sb = sb("qk_sb", [128, 768], F32)
    qk_rot = sb("qk_rot", [128, 768], BF16)
    rp = [sb(f"rp{i}", [128, 12, 8], F32) for i in range(4)]
    qT = sb("qT", [128, 4, 128], BF16)
    kT = [sb(f"kT{i}", [128, 2, 128], BF16) for i in range(2)]
    v_ext = [sb(f"vext{i}", [128, 2, 72], BF16) for i in range(2)]
    pT = sb("pT", [128, 16, 128], BF16)
    den_sb = sb("den_sb", [128, 4], F32)
    rden = sb("rden", [128, 4], F32)
    yatt = sb("yatt", [128, 8, 64], BF16)
    yT = sb("yT", [128, 8, 128], BF16)
    gc_sb = sb("gc_sb", [128, 2, 128], F32)
    gb_sb = sb("gb_sb", [128, 2, 128], F32)
    ubuf = [sb(f"ubuf{i}", [128, 130], F32) for i in range(4)]
    cacc = sb("cacc", [128, 2, 128], F32)
    st1 = sb("st1", [128, 12], F32)
    mv1 = sb("mv1", [128, 2], F32)
    sd1 = sb("sd1", [128, 1], F32)
    rstd1 = sb("rstd1", [128, 1], F32)
    nmr1 = sb("nmr1", [128, 1], F32)
    h1 = [sb(f"h1_{i}", [128, D], F32) for i in range(2)]
    h1b1 = sb("h1b1", [128, D], BF16)
    xb = h1b1
    h1T = xT
    qpT = pT
    vA = sb("vA", [128, 16, 16], F32)
    iA = sb("iA", [128, 16, 16], U32)
    sc = sb("sc", [128, 8, 16], F32)
    posu = sb("posu", [128, 8, 16], U32)
    pposf = sb("pposf", [128, 128], BF16)
    i12f = sb("i12f", [128, 16, 16], BF16)
    d12 = sb("d12", [128, 16, 16], BF16)

    _ptf = pT[:, :, :].rearrange("p a b -> p (a b)")
    age = _ptf[:, 0:1024].rearrange("p (k a) -> p k a", a=16)
    prod = _ptf[:, 1024:2048].rearrange("p (k a) -> p k a", a=16)
    sumA = sb("sumA", [128, 64], F32)
    bq = sb("bq", [128, 64], BF16)
    isel = sb("isel", [128, 2, 128], F32)
    ef = sb("ef", [128, 128], F32)
    e_i32 = [sb(f"e_i32_{i}", [128, 128], I32) for i in range(2)]
    scs = sb("scs", [128, 8, 16], F32)
    pex = sb("pex", [128, 8, 16], F32)
    ssum = sb("ssum", [128, 8], F32)
    rsum = sb("rsum", [128, 8], F32)
    gw = [sb(f"gw{i}", [128, 8, 16], F32) for i in range(2)]

    a_sb = sb("a_sb", [128, 128], F32)
    ga_sb = sb("ga_sb", [128, 128], F32)
    coef = sb("coef", [128, 128], F32)
    dg = [sb(f"dg{i}", [128, 128], BF16) for i in range(NDR)]
    r2 = sb("r2", [128, D], F32)
    st2 = sb("st2", [128, 12], F32)
    mv2 = sb("mv2", [128, 2], F32)
    sd2 = sb("sd2", [128, 1], F32)
    rstd2 = sb("rstd2", [128, 1], F32)
    nmr2 = sb("nmr2", [128, 1], F32)

    ps = es.enter_context(nc.psum_tensor("ps", [128, 4096], F32))
    tp_bf = ps[:, 0:512].bitcast(BF16)

    def bank(k, c0=0, n=512):
        return ps[:, k * 512 + c0: k * 512 + c0 + n]

    def newsem(name):
        return es.enter_context(nc.semaphore(name))

    eng_sems = {e: newsem("sem_" + e) for e in ("pe", "act", "dve", "pool")}
    eng_sems["sp"] = None
    ds_setup_pool = DSem(newsem("ds_setup_pool"), group=True)
    ds_w_in = DSem(newsem("ds_w_in"), group=True)
    ds_w_out = DSem(newsem("ds_w_out"), group=True)
    ds_w_pq = DSem(newsem("ds_w_pq"), group=True)
    ds_setup_sp = DSem(newsem("ds_setup_sp"), group=True)
    ds_x = [DSem(newsem(f"ds_x{i}")) for i in range(1)]
    ds_g = [DSem(newsem(f"ds_g{i}")) for i in range(NG)]
    NPREP = 16
    ds_prep = [DSem(newsem(f"ds_prep{i}")) for i in range(NPREP)]
    ds_out = DSem(newsem("ds_out"))
    ds_tap = DSem(newsem("ds_tap"))

    S = Sched()

    L_setup = S.begin()
    wv = w_in_d.rearrange("(c p) n -> p c n", p=128)
    for c in range(8):
        for (c0, c1) in ((0, 1216), (1216, 2432)):
            S.op("pool", lambda e, c=c, c0=c0, c1=c1: e.dma_start(out=w_in_sb[:, c, c0:c1], in_=wv[:, c, c0:c1]),
                 w=[("w_in", c, c0)], dsem=ds_w_in)
    S.op("pool", lambda e: e.dma_start(out=ident_bf[:, :], in_=ident_d), w=["ident_bf"], dsem=ds_setup_pool)
    S.op("pool", lambda e: e.dma_start(out=masks_bf[:, :, :], in_=masks_d.rearrange("p (m k) -> p m k", m=3)),
         w=["masks"], dsem=ds_setup_pool)
    S.op("pool", lambda e: e.dma_start(out=btm_bf[:, :], in_=btm_d), w=["btm"], dsem=ds_setup_pool)
    S.op("pool", lambda e: e.dma_start(out=bout_bf[:, :], in_=bout_d), w=["bout"], dsem=ds_setup_pool)
    S.op("pool", lambda e: e.dma_start(out=skT_sb[:, :, :], in_=skT_d.rearrange("p (s n) -> p s n", s=2)),
         w=["skT"], dsem=ds_setup_pool)
    wov = w_out_d.rearrange("(c p) n -> p c n", p=128)
    for c in range(8):
        S.op("pool", lambda e, c=c: e.dma_start(out=w_out_sb[:, c, :], in_=wov[:, c, :]),
             w=[("w_out", c)], dsem=ds_w_out)
    wpv = w_pq_d.rearrange("(c p) n -> p c n", p=128)
    for c in range(8):
        S.op("pool", lambda e, c=c: e.dma_start(out=w_pq_sb[:, c, :], in_=wpv[:, c, :]),
             w=[("w_pq", c)], dsem=ds_w_pq)
    PREP_ROWS = 512
    for i in range(NEXP // PREP_ROWS):
        S.op("pool", lambda e, i=i: e.dma_start(out=uvb_d[i * PREP_ROWS:(i + 1) * PREP_ROWS, :],
                                                in_=uv_d[i * PREP_ROWS:(i + 1) * PREP_ROWS, :]),
             w=[("uvb", i)], dsem=ds_prep[i % NPREP],
             xw=([(ds_prep[i % NPREP], 16 * (i // NPREP))] if i >= NPREP else []))
    ALLUVB = [("uvb", i) for i in range(NEXP // PREP_ROWS)]
    ALLW = [("w_in", c, c0) for c in range(8) for c0 in (0, 1216)]
    ALLWO = [("w_out", c) for c in range(8)]
    ALLWP = [("w_pq", c) for c in range(8)]

    S.op("sp", lambda e: e.dma_start(out=cst_sb[:, :], in_=cst_d), w=["cst"], dsem=ds_setup_sp)
    S.op("sp", lambda e: e.dma_start(out=pos_i[:, :], in_=pos_d), w=["pos_i"], dsem=ds_setup_sp)
    S.op("sp", lambda e: e.dma_start(out=bconv_sb[:, :], in_=bconv_d), w=["bconv"], dsem=ds_setup_sp)
    S.op("sp", lambda e: e.dma_start(out=cw_sb[:, :], in_=cw_d), w=["cw"], dsem=ds_setup_sp)
    S.op("sp", lambda e: e.dma_start(out=sinks_sb[:, :], in_=sinks_d.partition_broadcast(128)),
         w=["sinks"], dsem=ds_setup_sp)
    for i in range(4):
        S.op("sp", lambda e, i=i: e.dma_start(out=vecs_sb[:, i, :], in_=vecs_d[i, :].partition_broadcast(128)),
             w=[("vecs", i)], dsem=ds_setup_sp)

    S.op("dve", lambda e: e.memset(ones_bf[:, :], 1.0), w=["ones"])
    S.op("dve", lambda e: e.tensor_copy(out=thr_bf[:, :], in_=cst_sb[:, 8:40]), r=["cst"], w=["thr_bf"])
    S.op("dve", lambda e: e.memset(eps_t[:, :], LN_EPS), w=["eps"])
    for i in range(2):
        S.op("dve", lambda e, i=i: e.memset(v_ext[i][:, :, :], 1.0), w=[("vext", i)])
    S.op("dve", lambda e: e.tensor_copy(out=posf[:, :], in_=pos_i[:, :]), r=["pos_i"], w=["posf"])
    S.op("dve", lambda e: e.tensor_tensor(
        out=ang[:, :, :], in0=posf[:, :].unsqueeze(2).to_broadcast([128, NBLK + 1, 8]),
        in1=invf.unsqueeze(1).to_broadcast([128, NBLK + 1, 8]), op=ALU.mult), r=["posf", "cst"], w=["ang"])
    S.op("dve", lambda e: e.tensor_scalar(out=rtmp[:, :, :], in0=ang[:, :, :], scalar1=1.0 / TWO_PI, scalar2=None,
                                          op0=ALU.mult), r=["ang"], w=["rtmp"])
    S.op("dve", lambda e: e.tensor_copy(out=rki[:, :, :], in_=rtmp[:, :, :]), r=["rtmp"], w=["rki"])
    S.op("dve", lambda e: e.tensor_copy(out=rtmp[:, :, :], in_=rki[:, :, :]), r=["rki"], w=["rtmp"])
    S.op("dve", lambda e: e.scalar_tensor_tensor(out=rtmp2[:, :, :], in0=rtmp[:, :, :], scalar=-CW1, in1=ang[:, :, :],
                                                 op0=ALU.mult, op1=ALU.add), r=["rtmp", "ang"], w=["rtmp2"])
    S.op("dve", lambda e: e.scalar_tensor_tensor(out=ang[:, :, :], in0=rtmp[:, :, :], scalar=-CW2, in1=rtmp2[:, :, :],
                                                 op0=ALU.mult, op1=ALU.add), r=["rtmp", "rtmp2"], w=["ang"])

    def wrap(buf_key, buf):
        S.op("dve", lambda e: e.tensor_single_scalar(out=rtmp[:, :, :], in_=buf[:, :, :], scalar=np.pi, op=ALU.is_gt),
             r=[buf_key], w=["rtmp"])
        S.op("dve", lambda e: e.scalar_tensor_tensor(out=buf[:, :, :], in0=rtmp[:, :, :], scalar=-TWO_PI, in1=buf[:, :, :],
                                                     op0=ALU.mult, op1=ALU.add), r=["rtmp", buf_key], w=[buf_key])
        S.op("dve", lambda e: e.tensor_single_scalar(out=rtmp[:, :, :], in_=buf[:, :, :], scalar=-np.pi, op=ALU.is_lt),
             r=[buf_key], w=["rtmp"])
        S.op("dve", lambda e: e.scalar_tensor_tensor(out=buf[:, :, :], in0=rtmp[:, :, :], scalar=TWO_PI, in1=buf[:, :, :],
                                                     op0=ALU.mult, op1=ALU.add), r=["rtmp", buf_key], w=[buf_key])
        S.op("dve", lambda e: e.tensor_scalar(out=buf[:, :, :], in0=buf[:, :, :], scalar1=PI_SAFE, scalar2=-PI_SAFE,
                                              op0=ALU.min, op1=ALU.max), r=[buf_key], w=[buf_key])

    wrap("ang", ang)
    S.op("act", lambda e: e.activation(out=sin_t[:, :, :], in_=ang[:, :, :], func=AF.Sin), r=["ang"], w=["sin_t"])
    S.op("dve", lambda e: e.tensor_scalar(out=rtmp2[:, :, :], in0=ang[:, :, :], scalar1=float(np.pi / 2), scalar2=None,
                                          op0=ALU.add), r=["ang"], w=["rtmp2"])
    wrap("rtmp2", rtmp2)
    S.op("act", lambda e: e.activation(out=cos_t[:, :, :], in_=rtmp2[:, :, :], func=AF.Sin), r=["rtmp2"], w=["cos_t"])
    S.op("act", lambda e: e.activation(out=esink[:, :], in_=sinks_sb[:, :], func=AF.Exp), r=["sinks"], w=["esink"])

    def phase_a(b):
        L = S.begin()
        xs = 0
        kv = b % 2
        kvp = (b - 1) % 2
        hs = b % 2
        S.op("sp", lambda e: e.dma_start(out=xf[xs][:, :], in_=xh[b * 128:(b + 1) * 128, :]), w=[("xf", xs)], dsem=ds_x[xs])
        S.op("act", lambda e: e.activation(out=xb[:, :], in_=xf[xs][:, :], func=AF.Copy), r=[("xf", xs)], w=["h1b"])
        for c in range(8):
            S.op("pe", lambda e, c=c: e.transpose(out=tp_bf[:, c * 128:(c + 1) * 128], in_=xb[:, c * 128:(c + 1) * 128],
                                                  identity=ident_bf[:, :]), r=["h1b", "ident_bf"], w=[("ps", 0)])
        S.op("act", lambda e: e.activation(out=xT[:, :, :], in_=tp_bf[:, :].rearrange("p (c t) -> p c t", c=8), func=AF.Copy),
             r=[("ps", 0)], w=["xT"])
        groups = []
        if b >= 1:
            groups.append((1, 0, 512))
        groups.append((2, 512, 384))
        for (bk, c0, n) in groups:
            for c in range(8):
                S.op("pe", lambda e, c=c, bk=bk, c0=c0, n=n: e.matmul(
                    out=bank(bk, 0, n), lhsT=xT[:, c, :], rhs=w_in_sb[:, c, c0:c0 + n], start=(c == 0), stop=False),
                    r=["xT"] + ALLW, w=[("ps", bk)])
            S.op("pe", lambda e, bk=bk, c0=c0, n=n: e.matmul(
                out=bank(bk, 0, n), lhsT=ones_bf[0:1, :], rhs=btm_bf[0:1, c0:c0 + n], start=False, stop=True),
                r=["ones", "btm"], w=[("ps", bk)])
        kinds = (1, 2) if b == 0 else (0, 1, 2)
        for kind in kinds:
            bk = 3 + kind
            for cc in range(4):
                col = 896 + kind * 512 + cc * 128
                for c in range(8):
                    S.op("pe", lambda e, c=c, bk=bk, cc=cc, col=col: e.matmul(
                        out=bank(bk, cc * 128, 128), lhsT=w_in_sb[:, c, col:col + 128], rhs=xT[:, c, :],
                        start=(c == 0), stop=(c == 7)), r=["xT"] + ALLW, w=[("ps", bk)])
        if b >= 1:
            S.op("act", lambda e: e.activation(out=qk_sb[:, 0:512], in_=bank(1), func=AF.Copy), r=[("ps", 1)], w=["qk_q"])
        S.op("act", lambda e: e.activation(out=qk_sb[:, 512:768], in_=bank(2, 0, 256), func=AF.Copy), r=[("ps", 2)], w=["qk_k"])
        S.op("act", lambda e: e.activation(out=v_ext[kv][:, :, 0:64], in_=bank(2, 256, 128).rearrange("p (g d) -> p g d", g=2),
                                           func=AF.Copy), r=[("ps", 2)], w=[("vext", kv)])
        h0 = 0 if b >= 1 else 8
        nh = 12 - h0
        qk3 = qk_sb[:, :].rearrange("p (h d) -> p h d", d=64)
        qr3 = qk_rot[:, :].rearrange("p (h d) -> p h d", d=64)
        cosb = cos_t[:, b, :].unsqueeze(1).to_broadcast([128, nh, 8])
        sinb = sin_t[:, b, :].unsqueeze(1).to_broadcast([128, nh, 8])
        r1 = qk3[:, h0:12, 0:8]
        r2_ = qk3[:, h0:12, 8:16]
        qkr = ["qk_q", "qk_k"]
        S.op("act", lambda e: e.activation(out=qk_rot[:, h0 * 64:768], in_=qk_sb[:, h0 * 64:768], func=AF.Copy), r=qkr, w=["qk_rot"])
        S.op("dve", lambda e: e.tensor_tensor(out=rp[0][:, h0:12, :], in0=r1, in1=cosb, op=ALU.mult), r=qkr + ["cos_t"], w=["rp0"])
        S.op("dve", lambda e: e.tensor_tensor(out=rp[1][:, h0:12, :], in0=r2_, in1=sinb, op=ALU.mult), r=qkr + ["sin_t"], w=["rp1"])
        S.op("dve", lambda e: e.tensor_tensor(out=rp[2][:, h0:12, :], in0=r2_, in1=cosb, op=ALU.mult), r=qkr + ["cos_t"], w=["rp2"])
        S.op("dve", lambda e: e.tensor_tensor(out=rp[3][:, h0:12, :], in0=r1, in1=sinb, op=ALU.mult), r=qkr + ["sin_t"], w=["rp3"])
        S.op("dve", lambda e: e.tensor_tensor(out=qr3[:, h0:12, 0:8], in0=rp[0][:, h0:12, :], in1=rp[1][:, h0:12, :], op=ALU.subtract),
             r=["rp0", "rp1"], w=["qk_rot"])
        S.op("dve", lambda e: e.tensor_tensor(out=qr3[:, h0:12, 8:16], in0=rp[2][:, h0:12, :], in1=rp[3][:, h0:12, :], op=ALU.add),
             r=["rp2", "rp3"], w=["qk_rot"])
        jlist = list(range(4, 6)) if b == 0 else list(range(6))
        for j in jlist:
            S.op("pe", lambda e, j=j: e.transpose(out=tp_bf[:, j * 128:(j + 1) * 128], in_=qk_rot[:, j * 128:(j + 1) * 128],
                                                  identity=ident_bf[:, :]), r=["qk_rot", "ident_bf"], w=[("ps", 0)])
        if b >= 1:
            S.op("act", lambda e: e.activation(out=qT[:, :, :], in_=tp_bf[:, 0:512].rearrange("p (c t) -> p c t", c=4), func=AF.Copy),
                 r=[("ps", 0)], w=["qT"])
        S.op("act", lambda e: e.activation(out=kT[kv][:, :, :], in_=tp_bf[:, 512:768].rearrange("p (c t) -> p c t", c=2), func=AF.Copy),
             r=[("ps", 0)], w=[("kT", kv)])

        for cc in range(4):
            S.op("act", lambda e, cc=cc: e.activation(out=gc_sb[:, cc % 2, :], in_=bank(4, cc * 128, 128), func=AF.Identity,
                                                      bias=bconv_sb[:, 4 + cc:5 + cc], scale=1.0),
                 r=[("ps", 4), "bconv"], w=[("gc", cc % 2)])
            if b >= 1:
                S.op("dve", lambda e, cc=cc: e.tensor_copy(out=ubuf[cc][:, 0:2], in_=ubuf[cc][:, 128:130]),
                     r=[("u", cc)], w=[("uh", cc)])
            S.op("dve", lambda e, cc=cc: e.scalar_tensor_tensor(
                out=ubuf[cc][:, 2:130], in0=bank(5, cc * 128, 128), scalar=bconv_sb[:, 8 + cc:9 + cc], in1=gc_sb[:, cc % 2, :],
                op0=ALU.add, op1=ALU.mult), r=[("ps", 5), ("gc", cc % 2), "bconv", ("uh", cc)], w=[("u", cc)])
            if b == 0:
                S.op("dve", lambda e, cc=cc: e.tensor_scalar(out=ubuf[cc][:, 2:130], in0=ubuf[cc][:, 2:130], scalar1=flag,
                                                             scalar2=None, op0=ALU.mult), r=[("u", cc), "cst"], w=[("u", cc)])
            else:
                S.op("act", lambda e, cc=cc: e.activation(out=gb_sb[:, cc % 2, :], in_=bank(3, cc * 128, 128), func=AF.Identity,
                                                          bias=bconv_sb[:, cc:cc + 1], scale=1.0),
                     r=[("ps", 3), "bconv"], w=[("gb", cc % 2)])
                S.op("dve", lambda e, cc=cc: e.tensor_scalar(out=cacc[:, cc % 2, :], in0=ubuf[cc][:, 0:128],
                                                             scalar1=cw_sb[:, cc * 3:cc * 3 + 1], scalar2=None, op0=ALU.mult),
                     r=[("u", cc), ("uh", cc), "cw"], w=[("cacc", cc % 2)])
                S.op("dve", lambda e, cc=cc: e.scalar_tensor_tensor(
                    out=cacc[:, cc % 2, :], in0=ubuf[cc][:, 1:129], scalar=cw_sb[:, cc * 3 + 1:cc * 3 + 2], in1=cacc[:, cc % 2, :],
                    op0=ALU.mult, op1=ALU.add), r=[("u", cc), ("uh", cc), ("cacc", cc % 2)], w=[("cacc", cc % 2)])
                S.op("dve", lambda e, cc=cc: e.scalar_tensor_tensor(
                    out=cacc[:, cc % 2, :], in0=ubuf[cc][:, 2:130], scalar=cw_sb[:, cc * 3 + 2:cc * 3 + 3], in1=cacc[:, cc % 2, :],
                    op0=ALU.mult, op1=ALU.add), r=[("u", cc), ("cacc", cc % 2)], w=[("cacc", cc % 2)])
                S.op("dve", lambda e, cc=cc: e.tensor_tensor(out=yT[:, 4 + cc, :], in0=cacc[:, cc % 2, :], in1=gb_sb[:, cc % 2, :], op=ALU.mult),
                     r=[("cacc", cc % 2), ("gb", cc % 2)], w=[("yT", 4 + cc)])
        if b == 0:
            return L

        mP = 2 if b == 1 else 0
        for h in range(8):
            g, j, base = h // 4, h // 2, (h % 2) * 64
            for (which, bk, msk, kslot) in ((0, 1 + 2 * g, mP, kvp), (1, 2 + 2 * g, 1, kv)):
                reg = bank(bk, (h % 4) * 128, 128)
                S.op("pe", lambda e, reg=reg, msk=msk: e.matmul(out=reg, lhsT=ident_bf[:, :], rhs=masks_bf[:, msk, :],
                                                                start=True, stop=False), r=["ident_bf", "masks"], w=[("ps", bk)])
                S.op("pe", lambda e, reg=reg, kslot=kslot, g=g, j=j, base=base: e.matmul(
                    out=reg, lhsT=kT[kslot][base:base + 64, g, :], rhs=qT[base:base + 64, j, :], start=False, stop=True),
                    r=[("kT", kslot), "qT"], w=[("ps", bk)])
        for bk in range(1, 5):
            S.op("act", lambda e, bk=bk: e.activation(out=pT[:, (bk - 1) * 4:bk * 4, :],
                                                      in_=bank(bk).rearrange("p (h t) -> p h t", h=4), func=AF.Exp, scale=0.125),
                 r=[("ps", bk)], w=[("pT", bk)])
        for g in range(2):
            for hh in range(4):
                reg = bank(5, hh * 65, 65)
                S.op("pe", lambda e, reg=reg, g=g, hh=hh: e.matmul(out=reg, lhsT=pT[:, (2 * g) * 4 + hh, :],
                                                                   rhs=v_ext[kvp][:, g, 0:65], start=True, stop=False),
                     r=[("pT", 1 + 2 * g), ("vext", kvp)], w=[("ps", 5)])
                S.op("pe", lambda e, reg=reg, g=g, hh=hh: e.matmul(out=reg, lhsT=pT[:, (2 * g + 1) * 4 + hh, :],
                                                                   rhs=v_ext[kv][:, g, 0:65], start=False, stop=True),
                     r=[("pT", 2 + 2 * g), ("vext", kv)], w=[("ps", 5)])
            o3 = bank(5, 0, 260).rearrange("p (h d) -> p h d", d=65)
            S.op("dve", lambda e, g=g, o3=o3: e.tensor_tensor(out=den_sb[:, :], in0=o3[:, :, 64], in1=esink[:, g * 4:(g + 1) * 4], op=ALU.add),
                 r=[("ps", 5), "esink"], w=["den"])
            S.op("dve", lambda e: e.reciprocal(out=rden[:, :], in_=den_sb[:, :]), r=["den"], w=["rden"])
            S.op("dve", lambda e, g=g, o3=o3: e.tensor_tensor(out=yatt[:, g * 4:(g + 1) * 4, :], in0=o3[:, :, 0:64],
                                                              in1=rden[:, :].unsqueeze(2).to_broadcast([128, 4, 64]), op=ALU.mult),
                 r=[("ps", 5), "rden"], w=[("yatt", g)])
        yatt2 = yatt[:, :, :].rearrange("p h d -> p (h d)")
        for j in range(4):
            S.op("pe", lambda e, j=j: e.transpose(out=tp_bf[:, j * 128:(j + 1) * 128], in_=yatt2[:, j * 128:(j + 1) * 128],
                                                  identity=ident_bf[:, :]), r=[("yatt", 0), ("yatt", 1), "ident_bf"], w=[("ps", 0)])
        S.op("act", lambda e: e.activation(out=yT[:, 0:4, :], in_=tp_bf[:, 0:512].rearrange("p (c t) -> p c t", c=4), func=AF.Copy),
             r=[("ps", 0)], w=[("yT", c) for c in range(4)])
        for half in range(2):
            bk = 1 + half
            for c in range(8):
                S.op("pe", lambda e, c=c, bk=bk, half=half: e.matmul(out=bank(bk), lhsT=yT[:, c, :],
                                                                     rhs=w_out_sb[:, c, half * 512:(half + 1) * 512],
                                                                     start=(c == 0), stop=False),
                     r=[("yT", c)] + ALLWO, w=[("ps", bk)])
            S.op("pe", lambda e, bk=bk, half=half: e.matmul(out=bank(bk), lhsT=ones_bf[0:1, :],
                                                            rhs=bout_bf[0:1, half * 512:(half + 1) * 512], start=False, stop=True),
                 r=["ones", "bout"], w=[("ps", bk)])
        rbuf = xf[xs]
        S.op("dve", lambda e: e.scalar_tensor_tensor(out=rbuf[:, :], in0=xf[xs][:, :], scalar=ALPHA, in1=ps[:, 512:1536],
                                                     op0=ALU.mult, op1=ALU.add), r=[("xf", xs), ("ps", 1), ("ps", 2)], w=[("xf", xs)])
        layer_norm(S, "1", rbuf, ("xf", xs), st1, mv1, sd1, rstd1, nmr1, vecs_sb[:, 0, :], vecs_sb[:, 1, :], [("vecs", 0), ("vecs", 1)],
                   h1[hs], ("h1", hs))
        S.op("act", lambda e: e.activation(out=h1b1[:, :], in_=h1[hs][:, :], func=AF.Copy), r=[("h1", hs)], w=["h1b"])
        for c in range(8):
            S.op("pe", lambda e, c=c: e.transpose(out=tp_bf[:, c * 128:(c + 1) * 128], in_=h1b1[:, c * 128:(c + 1) * 128],
                                                  identity=ident_bf[:, :]), r=["h1b", "ident_bf"], w=[("ps", 0)])
        S.op("act", lambda e: e.activation(out=h1T[:, :, :], in_=tp_bf[:, :].rearrange("p (c t) -> p c t", c=8), func=AF.Copy),
             r=[("ps", 0)], w=["xT"])
        for jj in range(16):
            bk = 1 + jj // 4
            for c in range(8):
                S.op("pe", lambda e, c=c, jj=jj, bk=bk: e.matmul(out=bank(bk, (jj % 4) * 128, 128),
                                                                 lhsT=w_pq_sb[:, c, jj * 128:(jj + 1) * 128], rhs=h1T[:, c, :],
                                                                 start=(c == 0), stop=(c == 7)),
                     r=["xT"] + ALLWP, w=[("ps", bk)])
        for bk in range(1, 5):
            S.op("act", lambda e, bk=bk: e.activation(out=qpT[:, (bk - 1) * 4:bk * 4, :],
                                                      in_=bank(bk).rearrange("p (h t) -> p h t", h=4), func=AF.Copy),
                 r=[("ps", bk)], w=[("pT", bk)])
        for jj in range(16):
            bk = 1 + jj // 4
            S.op("pe", lambda e, jj=jj, bk=bk: e.matmul(out=bank(bk, (jj % 4) * 128, 128), lhsT=qpT[:, jj, :],
                                                        rhs=skT_sb[:, jj % 2, :], start=True, stop=True),
                 r=[("pT", bk), "skT"], w=[("ps", bk)])

        def top16(src_fn, key, vdst, idst, vkey, ikey):
            S.op("dve", lambda e: e.max(out=vdst[:, 0:8], in_=src_fn()), r=[key], w=[vkey])
            S.op("dve", lambda e: e.max_index(out=idst[:, 0:8], in_max=vdst[:, 0:8], in_values=src_fn()), r=[key, vkey], w=[ikey])
            S.op("dve", lambda e: e.match_replace(out=src_fn(), in_to_replace=vdst[:, 0:8], in_values=src_fn(), imm_value=-1e30),
                 r=[vkey, ikey, key], w=[key])
            S.op("dve", lambda e: e.max(out=vdst[:, 8:16], in_=src_fn()), r=[key], w=[vkey])
            S.op("dve", lambda e: e.max_index(out=idst[:, 8:16], in_max=vdst[:, 8:16], in_values=src_fn()), r=[key, vkey], w=[ikey])

        for jj in range(16):
            top16(lambda jj=jj: bank(1 + jj // 4, (jj % 4) * 128, 128), ("ps", 1 + jj // 4), vA[:, jj, :], iA[:, jj, :], ("vA", jj), ("iA", jj))
        ALLS = [("ps", bk) for bk in range(1, 5)]
        ALLVA = [("vA", jj) for jj in range(16)]
        ALLIA = [("iA", jj) for jj in range(16)]
        vA4 = vA[:, :, :].rearrange("p (h s) k -> p h s k", s=2)
        iA4 = iA[:, :, :].rearrange("p (h s) k -> p h s k", s=2)
        cand = ps[:, 512:2560].rearrange("p (h c) -> p h c", c=256)
        cand4 = cand.rearrange("p h (a b) -> p h a b", b=16)
        S.op("dve", lambda e: e.tensor_tensor(out=cand4, in0=vA4[:, :, 0, :].unsqueeze(3).to_broadcast([128, 8, 16, 16]),
                                              in1=vA4[:, :, 1, :].unsqueeze(2).to_broadcast([128, 8, 16, 16]), op=ALU.add),
             r=ALLVA + ALLS, w=ALLS)
        for h in range(8):
            top16(lambda h=h: cand[:, h, :], ("ps", 1 + h // 2), sc[:, h, :], posu[:, h, :], ("sc", h), ("posu", h))
        ALLSC = [("sc", h) for h in range(8)]
        ALLPOS = [("posu", h) for h in range(8)]
        S.op("dve", lambda e: e.tensor_copy(out=pposf[:, :], in_=posu[:, :, :].rearrange("p h k -> p (h k)")), r=ALLPOS, w=["pposf"])
        S.op("dve", lambda e: e.tensor_copy(out=i12f[:, :, :], in_=iA[:, :, :]), r=ALLIA, w=["i12f"])
        S.op("dve", lambda e: e.tensor_copy(out=d12[:, :, 0:1], in_=i12f[:, :, 0:1]), r=["i12f"], w=["d12a"])
        S.op("dve", lambda e: e.tensor_tensor(out=d12[:, :, 1:16], in0=i12f[:, :, 1:16], in1=i12f[:, :, 0:15], op=ALU.subtract),
             r=["i12f"], w=["d12b"])
        d4 = d12[:, :, :].rearrange("p (h s) k -> p h s k", s=2)
        for half in range(2):
            hk0 = half * 64
            age4 = age[:, :, :].rearrange("p (h k) a -> p h k a", h=4)
            prod4 = prod[:, :, :].rearrange("p (h k) a -> p h k a", h=4)
            S.op("dve", lambda e, hk0=hk0: e.tensor_tensor(
                out=age[:, :, :], in0=pposf[:, hk0:hk0 + 64].unsqueeze(2).to_broadcast([128, 64, 16]),
                in1=thrA.unsqueeze(1).to_broadcast([128, 64, 16]), op=ALU.is_ge), r=["pposf", "thr_bf"], w=["age", ("pT", 1), ("pT", 2)])
            S.op("dve", lambda e: e.tensor_reduce(out=sumA[:, :], in_=age[:, :, :], axis=AX.X, op=ALU.add), r=["age"], w=["sumA"])
            S.op("dve", lambda e, half=half, age4=age4, prod4=prod4: e.tensor_tensor(
                out=prod4, in0=age4, in1=d4[:, half * 4:(half + 1) * 4, 0, :].unsqueeze(2).to_broadcast([128, 4, 16, 16]),
                op=ALU.mult), r=["age", "d12a", "d12b"], w=["prod", ("pT", 3), ("pT", 4)])
            S.op("dve", lambda e, hk0=hk0: e.tensor_reduce(out=isel[:, 0, hk0:hk0 + 64], in_=prod[:, :, :], axis=AX.X, op=ALU.add),
                 r=["prod"], w=[("isel", 0, half)])
            S.op("dve", lambda e, hk0=hk0: e.scalar_tensor_tensor(out=bq[:, :], in0=sumA[:, :], scalar=-16.0, in1=pposf[:, hk0:hk0 + 64],
                                                                   op0=ALU.mult, op1=ALU.add), r=["sumA", "pposf"], w=["bq"])
            S.op("dve", lambda e: e.tensor_tensor(
                out=age[:, :, :], in0=bq[:, :].unsqueeze(2).to_broadcast([128, 64, 16]),
                in1=thrB.unsqueeze(1).to_broadcast([128, 64, 16]), op=ALU.is_ge), r=["bq", "thr_bf", "prod"], w=["age", ("pT", 1), ("pT", 2)])
            S.op("dve", lambda e, half=half, age4=age4, prod4=prod4: e.tensor_tensor(
                out=prod4, in0=age4, in1=d4[:, half * 4:(half + 1) * 4, 1, :].unsqueeze(2).to_broadcast([128, 4, 16, 16]),
                op=ALU.mult), r=["age", "d12a", "d12b"], w=["prod", ("pT", 3), ("pT", 4)])
            S.op("dve", lambda e, hk0=hk0: e.tensor_reduce(out=isel[:, 1, hk0:hk0 + 64], in_=prod[:, :, :], axis=AX.X, op=ALU.add),
                 r=["prod"], w=[("isel", 1, half)])
        es_ = b % 2
        S.op("dve", lambda e: e.scalar_tensor_tensor(out=ef[:, :], in0=isel[:, 0, :], scalar=128.0, in1=isel[:, 1, :],
                                                     op0=ALU.mult, op1=ALU.add),
             r=[("isel", s_, h_) for s_ in range(2) for h_ in range(2)], w=["ef"])
        S.op("dve", lambda e: e.tensor_copy(out=e_i32[es_][:, :], in_=ef[:, :]), r=["ef"], w=[("e_i32", es_)])
        S.op("dve", lambda e: e.tensor_tensor(out=scs[:, :, :], in0=sc[:, :, :], in1=sc[:, :, 0:1].to_broadcast([128, 8, 16]),
                                              op=ALU.subtract), r=ALLSC, w=["scs"])
        S.op("act", lambda e: e.activation(out=pex[:, :, :], in_=scs[:, :, :], func=AF.Exp), r=["scs"], w=["pex"])
        S.op("dve", lambda e: e.tensor_reduce(out=ssum[:, :], in_=pex[:, :, :], axis=AX.X, op=ALU.add), r=["pex"], w=["ssum"])
        S.op("dve", lambda e: e.reciprocal(out=rsum[:, :], in_=ssum[:, :]), r=["ssum"], w=["rsum"])
        S.op("dve", lambda e: e.tensor_tensor(out=gw[es_][:, :, :], in0=pex[:, :, :],
                                              in1=rsum[:, :].unsqueeze(2).to_broadcast([128, 8, 16]), op=ALU.mult),
             r=["pex", "rsum"], w=[("gw", es_)])
        return L

    cnt = {"gu": 0, "dg": 0}
    LDEF = {}

    def phase_b(b):
        L = S.begin()
        hs = b % 2
        es_ = b % 2
        gw2 = gw[es_][:, :, :].rearrange("p h k -> p (h k)")
        ring = {}

        def st_gather(s):
            r = cnt["gu"] % NG
            cnt["gu"] += 1
            ring[s] = r
            S.op("pool", lambda e, s=s, r=r: e.indirect_dma_start(
                out=guv[r][:, :], out_offset=None, in_=uvb_d,
                in_offset=bass.IndirectOffsetOnAxis(ap=e_i32[es_][:, s:s + 1], axis=0)),
                r=[("e_i32", es_)] + ALLUVB, w=[("gu", r), ("gv", r)], dsem=ds_g[r])

        def st_dot(s):
            r = ring[s]
            S.op("dve", lambda e, s=s, r=r: e.scalar_tensor_tensor(
                out=guv[r][:, 0:D], in0=guv[r][:, 0:D], scalar=1.0, in1=h1[hs][:, :], op0=ALU.mult, op1=ALU.mult,
                accum_out=a_sb[:, s:s + 1]), r=[("gu", r), ("h1", hs)], w=[("gu", r), ("a", s)])

        def st_gelu(s):
            S.op("act", lambda e, s=s: e.activation(out=ga_sb[:, s:s + 1], in_=a_sb[:, s:s + 1], func=AF.Gelu),
                 r=[("a", s)], w=[("ga", s)])

        def st_coef(s):
            S.op("act", lambda e, s=s: e.activation(out=coef[:, s:s + 1], in_=ga_sb[:, s:s + 1], func=AF.Identity,
                                                    scale=gw2[:, s:s + 1]),
                 r=[("ga", s), ("gw", es_)], w=[("coef", s)])

        def st_diag(s):
            rd = cnt["dg"] % NDR
            cnt["dg"] += 1
            ring[("d", s)] = rd
            S.op("act", lambda e, s=s, rd=rd: e.activation(out=dg[rd][:, :], in_=ident_bf[:, :], func=AF.Identity,
                                                           scale=coef[:, s:s + 1]),
                 r=[("coef", s), "ident_bf"], w=[("dg", rd)])

        def st_mm(s):
            r = ring[s]
            rd = ring[("d", s)]
            for half in range(2):
                S.op("pe", lambda e, s=s, rd=rd, r=r, half=half: e.matmul(
                    out=bank(6 + half), lhsT=dg[rd][:, :], rhs=guv[r][:, D + half * 512:D + (half + 1) * 512],
                    start=(s == 0), stop=(s == 127)), r=[("dg", rd), ("gv", r)], w=[("ps", 6 + half)])

        for s in range(128 + 2):
            if s < 128:
                st_gather(s)
                st_dot(s)
            if 0 <= s - 1 < 128:
                st_gelu(s - 1)
            if 0 <= s - 2 < 128:
                st_diag(s - 2)
                st_mm(s - 2)
            if 0 <= s - 1 < 128:
                st_coef(s - 1)
        if any(t[0] == "ff" for t in taps):
            S.op("dve", lambda e: e.tensor_copy(out=h1[1 - hs][:, :], in_=ps[:, 3072:4096]), r=[("ps", 6), ("ps", 7)], w=[("h1", 1 - hs)])
        S.op("dve", lambda e: e.scalar_tensor_tensor(out=r2[:, :], in0=h1[hs][:, :], scalar=ALPHA, in1=ps[:, 3072:4096],
                                                     op0=ALU.mult, op1=ALU.add), r=[("h1", hs), ("ps", 6), ("ps", 7)], w=["r2"])
        ldef = S.begin()
        layer_norm(S, "2", r2, "r2", st2, mv2, sd2, rstd2, nmr2, vecs_sb[:, 2, :], vecs_sb[:, 3, :], [("vecs", 2), ("vecs", 3)],
                   r2, "r2")
        S.op("sp", lambda e: e.dma_start(out=y_d[(b - 1) * 128:b * 128, :], in_=r2[:, :]), r=["r2"], dsem=ds_out)
        S.cur = L
        LDEF[b] = ldef
        return L

    def layer_norm(S, tag, buf, bkey, st, mv, sd, rstd, nmr, g_ap, b_ap, vkeys, dst, dkey):
        S.op("dve", lambda e: e.bn_stats(out=st[:, 0:6], in_=buf[:, 0:512]), r=[bkey], w=["st0" + tag])
        S.op("dve", lambda e: e.bn_stats(out=st[:, 6:12], in_=buf[:, 512:1024]), r=[bkey], w=["st1" + tag])
        S.op("dve", lambda e: e.bn_aggr(out=mv[:, :], in_=st[:, :]), r=["st0" + tag, "st1" + tag], w=["mv" + tag])
        S.op("act", lambda e: e.activation(out=sd[:, :], in_=mv[:, 1:2], func=AF.Sqrt, bias=eps_t[:, 0:1], scale=1.0),
             r=["mv" + tag, "eps"], w=["sd" + tag])
        S.op("dve", lambda e: e.reciprocal(out=rstd[:, :], in_=sd[:, :]), r=["sd" + tag], w=["rstd" + tag])
        S.op("dve", lambda e: e.tensor_scalar(out=nmr[:, :], in0=mv[:, 0:1], scalar1=rstd[:, 0:1], scalar2=-1.0,
                                              op0=ALU.mult, op1=ALU.mult), r=["mv" + tag, "rstd" + tag], w=["nmr" + tag])
        S.op("act", lambda e: e.activation(out=buf[:, :], in_=buf[:, :], func=AF.Identity, bias=nmr[:, 0:1], scale=rstd[:, 0:1]),
             r=[bkey, "nmr" + tag, "rstd" + tag], w=[bkey])
        S.op("dve", lambda e: e.tensor_tensor(out=buf[:, :], in0=buf[:, :], in1=g_ap, op=ALU.mult), r=[bkey, vkeys[0]], w=[bkey])
        S.op("dve", lambda e: e.tensor_tensor(out=dst[:, :], in0=buf[:, :], in1=b_ap, op=ALU.add), r=[bkey, vkeys[1]], w=[dkey])

    seq = list(L_setup)
    LA = [phase_a(b) for b in range(nblk + 1)]
    LB = {b: phase_b(b) for b in range(1, nblk + 1)}
    seq += LA[0]
    seq += LA[1]
    def with_deferred(lst, ldef, start=0.04, step=0.02):
        out = list(lst)
        n = len(lst)
        for i, o in enumerate(ldef):
            out.insert(min(len(out), int(n * (start + step * i)) + i), o)
        return out

    if nblk >= 2:
        seq += LA[2]
        seq += LB[1]
        for b in range(3, nblk + 1):
            seq += with_deferred(merge_lists(LA[b], LB[b - 1]), LDEF[b - 2])
        seq += with_deferred(LB[nblk], LDEF[nblk - 1])
    else:
        seq += LB[nblk]
    seq += LDEF[nblk]
    tap_src = {"h1": (h1[nblk % 2], ("h1", nblk % 2)), "e": (e_i32[nblk % 2], ("e_i32", nblk % 2)),
               "gw": (gw[nblk % 2], ("gw", nblk % 2)), "qk_rot": (qk_rot, "qk_rot"), "yT": (yT, ("yT", 0)),
               "a": (a_sb, ("a", 127)), "coef": (coef, ("coef", 127)), "vA": (vA, ("vA", 0)),
               "iA": (iA, ("iA", 0)), "sc": (sc, ("sc", 0)), "posu": (posu, ("posu", 0)), "cos": (cos_t, "cos_t"),
               "sin": (sin_t, "sin_t"), "qpT": (qpT, ("pT", 1)), "r2": (r2, "r2"), "isel": (isel, ("isel", 0, 0)), "ff": (h1[(nblk + 1) % 2], ("h1", (nblk + 1) % 2))}
    S.begin()
    for (tname, shape, dt) in taps:
        t_ap, key = tap_src[tname]
        dst = tap_d[tname]
        src = t_ap[:, :] if len(t_ap.shape) == 2 else t_ap[:, :, :].rearrange("p a b -> p (a b)")
        S.op("sp", lambda e, dst=dst, src=src: e.dma_start(out=dst, in_=src), r=[key], dsem=ds_tap)
    seq += S.cur

    resolve(seq, eng_sems)

    handles = {"pe": "tensor", "act": "scalar", "dve": "vector", "pool": "gpsimd", "sp": "sync"}
    with nc.Block() as block:
        def emit(engname):
            def body(e):
                waited = {}
                for o in seq:
                    if o.eng != engname:
                        continue
                    for (sem, val, grp) in o.waits:
                        if grp is not None:
                            val = grp.count
                        k = id(sem)
                        if waited.get(k, 0) < val:
                            e.wait_ge(sem, val)
                            waited[k] = val
                    inst = o.fn(e)
                    if o.dsem is not None:
                        inst.then_inc(o.dsem.sem, 16)
                    else:
                        inst.then_inc(eng_sems[engname], 1)
                if engname == "sp":
                    for dsm in (ds_out, ds_tap):
                        if dsm.count:
                            e.wait_ge(dsm.sem, dsm.count)
            return body
        block.tensor(emit("pe"))
        block.scalar(emit("act"))
        block.vector(emit("dve"))
        block.gpsimd(emit("pool"))
        block.sync(emit("sp"))
    es.close()
    return nc


_PROG_CACHE = {}


def prep_inputs(inputs, nblk=NBLK):
    x = np.asarray(inputs["x"], dtype=np.float32)
    positions = np.asarray(inputs["positions"], dtype=np.int32)
    w_in = np.asarray(inputs["w_in"], dtype=np.float32)[0]
    b_in = np.asarray(inputs["b_in"], dtype=np.float32)[0]
    cols = np.concatenate([np.arange(0, 512), np.arange(512, 576), np.arange(512, 576), np.arange(576, 640),
                           np.arange(576, 640), np.arange(640, 768), np.arange(768, 2304)])
    w_in_r = np.ascontiguousarray(w_in[:, cols])
    b_in_r = b_in[cols]
    b_tm = np.ascontiguousarray(b_in_r[None, 0:896])
    b_conv = np.ascontiguousarray(b_in_r[896:].reshape(12, 128).T)
    conv_w = np.asarray(inputs["conv_w"], dtype=np.float32)[0]
    cw = np.ascontiguousarray(conv_w.reshape(3, 4, 128).transpose(2, 1, 0).reshape(128, 12))
    sinks = np.ascontiguousarray(np.asarray(inputs["attn_sinks"], dtype=np.float32)[0])
    w_out = np.ascontiguousarray(np.asarray(inputs["w_out"], dtype=np.float32)[0])
    b_out = np.ascontiguousarray(np.asarray(inputs["b_out"], dtype=np.float32)[0][None, :])
    vecs = np.ascontiguousarray(np.stack([np.asarray(inputs[k], dtype=np.float32)[0]
                                          for k in ("ln1_g", "ln1_b", "ln2_g", "ln2_b")]))
    w_pq = np.ascontiguousarray(np.asarray(inputs["w_pq"], dtype=np.float32)[0])
    sk1 = np.asarray(inputs["sub_keys1"], dtype=np.float32)[0]
    sk2 = np.asarray(inputs["sub_keys2"], dtype=np.float32)[0]
    skT = np.ascontiguousarray(np.concatenate([sk1.T, sk2.T], axis=1))
    uv_exp = np.concatenate([np.asarray(inputs["u_experts"], dtype=np.float32)[0],
                             np.asarray(inputs["v_experts"], dtype=np.float32)[0]], axis=1)
    ident = np.eye(128, dtype=np.float32)
    kj = np.arange(128)[:, None]
    qi = np.arange(128)[None, :]
    maskP = np.where(kj > qi, 0.0, NEG).astype(np.float32)
    maskC = np.where(kj <= qi, 0.0, NEG).astype(np.float32)
    invf = (500000.0 ** (-np.arange(0, 16, 2, dtype=np.float32) / 16.0)).astype(np.float32)
    shared = dict(w_in_r=w_in_r, b_tm=b_tm, b_conv=b_conv, cw=cw, sinks=sinks, w_out=w_out, b_out=b_out, vecs=vecs,
                  w_pq=w_pq, skT=skT, uv_exp=uv_exp, ident=ident)
    in_maps = []
    for c in range(NCORES):
        bi, s0 = c // 4, (c % 4) * TPC
        first = (c % 4 == 0)
        nrows = (nblk + 1) * 128
        xh = np.zeros((nrows, D), dtype=np.float32)
        pp = np.zeros((NBLK + 1) * 128, dtype=np.int32)
        if first:
            xh[128:] = x[bi, s0:s0 + nblk * 128]
            pp[128:128 + nblk * 128] = positions[bi, s0:s0 + nblk * 128]
        else:
            xh[:] = x[bi, s0 - 128:s0 + nblk * 128]
            pp[:nrows] = positions[bi, s0 - 128:s0 + nblk * 128]
        pos = np.ascontiguousarray(pp.reshape(NBLK + 1, 128).T)
        maskP0 = np.full((128, 128), NEG, dtype=np.float32) if first else maskP
        masks = np.ascontiguousarray(np.concatenate([maskP, maskC, maskP0], axis=1))
        cst = np.zeros((128, 64), dtype=np.float32)
        cst[:, 0:8] = invf[None, :]
        cst[:, 8:24] = (16.0 * np.arange(16, dtype=np.float32))[None, :]
        cst[:, 24:40] = (np.arange(16, dtype=np.float32) - 16.0)[None, :]
        cst[:, 40] = 0.0 if first else 1.0
        m = dict(shared)
        m.update(xh=xh, pos=pos, masks=masks, cst=cst)
        in_maps.append(m)
    return in_maps


def kernel(**inputs):
    if "nc" not in _PROG_CACHE:
        _PROG_CACHE["nc"] = build_program()
    nc = _PROG_CACHE["nc"]
    in_maps = prep_inputs(inputs)
    res = run_bass_kernel_spmd(nc, in_maps, core_ids=list(range(NCORES)))
    out = np.empty((2, 8192, D), dtype=np.float32)
    for c in range(NCORES):
        bi, s0 = c // 4, (c % 4) * TPC
        out[bi, s0:s0 + TPC] = res.results[c]["y"]
    return out
```
